# Optimizing a Trainium2 kernel written in Bass

```python
import math
import jax, jax.numpy as jnp
from jax import lax
import numpy as np

D_MODEL = 1024
BATCH = 8
SEQ = 8192
DEPTH = 2

N_META = 16
BLOCK = 128
MIX_WIDTH = D_MODEL
N_RET_HEADS = 4
RET_DIM = MIX_WIDTH // 2 // N_RET_HEADS
N_DIFF_HEADS = 4
DIFF_V_DIM = MIX_WIDTH // 2 // N_DIFF_HEADS
DIFF_QK_DIM = DIFF_V_DIM // 2
N_SB_HEADS = 8
SB_DIM = MIX_WIDTH // N_SB_HEADS
D_FF = ((8 * D_MODEL // 3 + 127) // 128) * 128
CONV_WIDTH = 3
EPS = 1e-6
MASK_VALUE = -1e30
N_EVEN = (DEPTH + 1) // 2
N_ODD = DEPTH // 2
RET_W = N_RET_HEADS * RET_DIM
DIFF_QK_W = N_DIFF_HEADS * 2 * DIFF_QK_DIM
DIFF_V_W = N_DIFF_HEADS * DIFF_V_DIM
AB_IN = 4 * RET_W + 2 * DIFF_QK_W + DIFF_V_W
C_IN = 3 * N_SB_HEADS * SB_DIM

kernel_name = "hybrid_retention_diffattn_stickbreaking_convffn"


def rmsnorm(x, g):
    xf = x.astype(jnp.float32)
    y = xf * lax.rsqrt(jnp.mean(xf * xf, axis=-1, keepdims=True) + EPS)
    return y.astype(x.dtype) * g


def head_rmsnorm(x, g):
    xf = x.astype(jnp.float32)
    y = xf * lax.rsqrt(jnp.mean(xf * xf, axis=-1, keepdims=True) + EPS)
    return y.astype(x.dtype) * g.reshape(x.shape[-2:])


def head_groupnorm(x, g):
    xf = x.astype(jnp.float32)
    mu = jnp.mean(xf, axis=-1, keepdims=True)
    xc = xf - mu
    y = xc * lax.rsqrt(jnp.mean(xc * xc, axis=-1, keepdims=True) + EPS)
    return y.astype(x.dtype) * g.reshape(x.shape[-2:])


def to_blocks(t):
    b, p, h, d = t.shape
    return t.reshape(b, p // BLOCK, BLOCK, h, d).transpose(1, 0, 3, 2, 4)


def from_blocks(t):
    n, b, h, c, d = t.shape
    return t.transpose(1, 0, 3, 2, 4).reshape(b, n * c, h, d)


def retention(q, k, v, valid):
    b, p, h, dk = q.shape
    dv = v.shape[-1]
    k = jnp.where(valid[None, :, None, None], k * dk ** -0.5, 0)
    v = jnp.where(valid[None, :, None, None], v, 0)
    qc, kc, vc = to_blocks(q), to_blocks(k), to_blocks(v)
    log_g = jnp.log1p(-(2.0 ** (-5.0 - jnp.arange(h, dtype=jnp.float32))))
    j = jnp.arange(BLOCK, dtype=jnp.float32)
    rel = j[:, None] - j[None, :]
    decay = jnp.where(rel >= 0, jnp.exp(log_g[:, None, None] * jnp.maximum(rel, 0.0)), 0.0).astype(q.dtype)
    q_decay = jnp.exp(log_g[:, None] * (j + 1.0))[:, :, None].astype(q.dtype)
    k_decay = jnp.exp(log_g[:, None] * (BLOCK - 1.0 - j))[:, :, None].astype(q.dtype)
    chunk_decay = jnp.exp(log_g * BLOCK)[:, None, None].astype(q.dtype)
    scores = jnp.einsum('nbhqd,nbhkd->nbhqk', qc, kc) * decay
    inner = jnp.einsum('nbhqk,nbhke->nbhqe', scores, vc)
    kv = jnp.einsum('nbhkd,nbhke->nbhde', kc * k_decay, vc)

    def step(state, kv_n):
        return chunk_decay * state + kv_n, state

    _, prev = lax.scan(step, jnp.zeros((b, h, dk, dv), kv.dtype), kv)
    cross = jnp.einsum('nbhqd,nbhde->nbhqe', qc * q_decay, prev)
    return from_blocks(inner + cross)


def diff_attention(q1, q2, k1, k2, v, lam, valid):
    b, p, h, d = q1.shape
    n = p // BLOCK
    scale = d ** -0.5
    slopes = 2.0 ** (-8.0 * (jnp.arange(h, dtype=jnp.float32) + 1.0) / h)
    k1t, k2t, vt = (t.transpose(0, 2, 1, 3) for t in (k1, k2, v))
    key_pos = jnp.arange(p)

    def block(args):
        i, qb1, qb2 = args
        qpos = i * BLOCK + jnp.arange(BLOCK)
        dist = qpos[:, None] - key_pos[None, :]
        mask = (dist >= 0) & valid[None, :]
        bias = -slopes[:, None, None] * dist.astype(jnp.float32)

        def probs(qb, kt):
            s = jnp.einsum('bhqd,bhkd->bhqk', qb, kt).astype(jnp.float32) * scale + bias
            return jax.nn.softmax(jnp.where(mask, s, MASK_VALUE), axis=-1)

        a = probs(qb1, k1t) - lam * probs(qb2, k2t)
        return jnp.einsum('bhqk,bhke->bhqe', a.astype(vt.dtype), vt)

    out = lax.map(block, (jnp.arange(n), to_blocks(q1), to_blocks(q2)))
    return from_blocks(out)


def stick_breaking(q, k, v, valid):
    b, p, h, d = q.shape
    n = p // BLOCK
    scale = d ** -0.5
    kt, vt = k.transpose(0, 2, 1, 3), v.transpose(0, 2, 1, 3)
    key_pos = jnp.arange(p)

    def block(args):
        i, qb = args
        qpos = i * BLOCK + jnp.arange(BLOCK)
        mask = (qpos[:, None] > key_pos[None, :]) & valid[None, :]
        z = jnp.einsum('bhqd,bhkd->bhqk', qb, kt).astype(jnp.float32) * scale
        log_1m = jnp.where(mask, -jax.nn.softplus(z), 0.0)
        later = lax.cumsum(log_1m, axis=3, reverse=True) - log_1m
        w = jnp.where(mask, jnp.exp(jax.nn.log_sigmoid(z) + later), 0.0)
        return jnp.einsum('bhqk,bhke->bhqe', w.astype(vt.dtype), vt)

    out = lax.map(block, (jnp.arange(n), to_blocks(q)))
    return from_blocks(out)


def conv_ffn(h, w_up, w_conv, b_conv, w_down, valid):
    p = h.shape[1]
    gate, val = jnp.split(h @ w_up, 2, axis=-1)
    gate = jnp.where(valid[None, :, None], gate, 0)
    gp = jnp.pad(gate, ((0, 0), (CONV_WIDTH - 1, 0), (0, 0)))
    conv = b_conv + sum(gp[:, tap:tap + p] * w_conv[tap] for tap in range(CONV_WIDTH))
    return (jax.nn.silu(conv) * val) @ w_down


def mixer_ab(h, w_in, ret_norm, diff_norm, lam_q1, lam_k1, lam_q2, lam_k2, w_out, lambda_init, valid):
    b, p, _ = h.shape
    proj = h @ w_in
    cuts = [RET_W, 2 * RET_W, 3 * RET_W, 4 * RET_W, 4 * RET_W + DIFF_QK_W, 4 * RET_W + 2 * DIFF_QK_W]
    rq, rk, rv, rg, dq, dk, dv = jnp.split(proj, cuts, axis=-1)
    ret = retention(rq.reshape(b, p, N_RET_HEADS, RET_DIM), rk.reshape(b, p, N_RET_HEADS, RET_DIM),
                    rv.reshape(b, p, N_RET_HEADS, RET_DIM), valid)
    ret = head_groupnorm(ret, ret_norm).reshape(b, p, RET_W) * jax.nn.silu(rg)
    dq = dq.reshape(b, p, N_DIFF_HEADS, 2, DIFF_QK_DIM)
    dk = dk.reshape(b, p, N_DIFF_HEADS, 2, DIFF_QK_DIM)
    f32 = jnp.float32
    lam = (jnp.exp(jnp.sum(lam_q1.astype(f32) * lam_k1.astype(f32)))
           - jnp.exp(jnp.sum(lam_q2.astype(f32) * lam_k2.astype(f32))) + lambda_init)
    dif = diff_attention(dq[..., 0, :], dq[..., 1, :], dk[..., 0, :], dk[..., 1, :],
                         dv.reshape(b, p, N_DIFF_HEADS, DIFF_V_DIM), lam, valid)
    dif = (head_rmsnorm(dif, diff_norm) * (1.0 - lambda_init)).reshape(b, p, DIFF_V_W)
    return jnp.concatenate([ret, dif], axis=-1) @ w_out


def mixer_c(h, w_in, w_out, valid):
    b, p, _ = h.shape
    q, k, v = jnp.split(h @ w_in, 3, axis=-1)
    shp = (b, p, N_SB_HEADS, SB_DIM)
    o = stick_breaking(q.reshape(shp), k.reshape(shp), v.reshape(shp), valid)
    return o.reshape(b, p, MIX_WIDTH) @ w_out


def setup_inputs(seed: int = 0) -> dict:
    key = jax.random.key(seed)
    ks = jax.random.split(key, 20)

    def nrm(k, shape, scale):
        return jax.random.normal(k, shape, jnp.float32) * scale

    return {
        "x": nrm(ks[0], (BATCH, SEQ, D_MODEL), 1.0),
        "meta_tokens": nrm(ks[1], (N_META, D_MODEL), 1.0),
        "mix_norm": 1.0 + nrm(ks[2], (DEPTH, D_MODEL), 0.02),
        "ffn_norm": 1.0 + nrm(ks[3], (DEPTH, D_MODEL), 0.02),
        "ffn_up": nrm(ks[4], (DEPTH, D_MODEL, 2 * D_FF), D_MODEL ** -0.5),
        "ffn_conv": nrm(ks[5], (DEPTH, CONV_WIDTH, D_FF), CONV_WIDTH ** -0.5),
        "ffn_conv_b": nrm(ks[6], (DEPTH, D_FF), 0.01),
        "ffn_down": nrm(ks[7], (DEPTH, D_FF, D_MODEL), D_FF ** -0.5),
        "ab_w_in": nrm(ks[8], (N_EVEN, D_MODEL, AB_IN), D_MODEL ** -0.5),
        "ab_ret_norm": 1.0 + nrm(ks[9], (N_EVEN, RET_W), 0.02),
        "ab_diff_norm": 1.0 + nrm(ks[10], (N_EVEN, DIFF_V_W), 0.02),
        "ab_lam_q1": nrm(ks[11], (N_EVEN, DIFF_QK_DIM), 0.1),
        "ab_lam_k1": nrm(ks[12], (N_EVEN, DIFF_QK_DIM), 0.1),
        "ab_lam_q2": nrm(ks[13], (N_EVEN, DIFF_QK_DIM), 0.1),
        "ab_lam_k2": nrm(ks[14], (N_EVEN, DIFF_QK_DIM), 0.1),
        "ab_w_out": nrm(ks[15], (N_EVEN, MIX_WIDTH, D_MODEL), MIX_WIDTH ** -0.5),
        "c_w_in": nrm(ks[16], (N_ODD, D_MODEL, C_IN), D_MODEL ** -0.5),
        "c_w_out": nrm(ks[17], (N_ODD, MIX_WIDTH, D_MODEL), MIX_WIDTH ** -0.5),
        "final_norm": 1.0 + nrm(ks[18], (D_MODEL,), 0.02),
    }


def reference(x, meta_tokens, mix_norm, ffn_norm, ffn_up, ffn_conv, ffn_conv_b, ffn_down,
              ab_w_in, ab_ret_norm, ab_diff_norm, ab_lam_q1, ab_lam_k1, ab_lam_q2, ab_lam_k2, ab_w_out,
              c_w_in, c_w_out, final_norm):
    b = x.shape[0]
    pad = jnp.zeros((b, BLOCK - N_META, D_MODEL), x.dtype)
    meta = jnp.broadcast_to(meta_tokens[None].astype(x.dtype), (b, N_META, D_MODEL))
    h = jnp.concatenate([pad, meta, x], axis=1)
    p = h.shape[1]
    valid = jnp.arange(p) >= (BLOCK - N_META)
    for i in range(DEPTH):
        y = rmsnorm(h, mix_norm[i])
        if i % 2 == 0:
            e = i // 2
            lambda_init = 0.8 - 0.6 * math.exp(-0.3 * i)
            h = h + mixer_ab(y, ab_w_in[e], ab_ret_norm[e], ab_diff_norm[e], ab_lam_q1[e], ab_lam_k1[e],
                             ab_lam_q2[e], ab_lam_k2[e], ab_w_out[e], lambda_init, valid)
        else:
            o = i // 2
            h = h + mixer_c(y, c_w_in[o], c_w_out[o], valid)
        h = h + conv_ffn(rmsnorm(h, ffn_norm[i]), ffn_up[i], ffn_conv[i], ffn_conv_b[i], ffn_down[i], valid)
    return rmsnorm(h, final_norm)[:, BLOCK:, :]
```

```python
import contextlib
import math
import numpy as np
import concourse.bass as bass
import concourse.mybir as mybir
from concourse.bass_utils import run_bass_kernel_spmd

F32 = mybir.dt.float32
BF16 = mybir.dt.bfloat16
AF = mybir.ActivationFunctionType
ALU = mybir.AluOpType
AX = mybir.AxisListType

D = 1024
NMETA = 16
DFF = 2816
EPS = 1e-6
NEG = -30000.0


class Buf:
    __slots__ = ("w", "r", "rd")

    def __init__(self):
        self.w = None
        self.r = {}
        self.rd = []


class Op:
    __slots__ = ("eng", "fn", "deps", "needed", "ms", "dsem", "dkey", "dval", "phase", "isdma")


class Sched:
    CE = ("pe", "act", "dve", "pool")
    ALLE = ("pe", "act", "dve", "pool", "sp")

    def __init__(self, nc, es, nds=20):
        self.nc = nc
        self.sem = {e: es.enter_context(nc.semaphore("s_" + e)) for e in self.CE}
        self.dsems = [es.enter_context(nc.semaphore("dq%d" % i)) for i in range(nds)]
        self.dval = [0] * nds
        self.dlast = [None] * nds
        self.drr = 0
        self.cnt = {e: 0 for e in self.CE}
        self.waited = {e: {} for e in self.ALLE}
        self.ops = {e: [] for e in self.ALLE}
        self.phase = 0
        self.nops = 0

    def _add(self, op, reads, writes):
        deps = set()
        for b in reads:
            if b.w is not None:
                deps.add(b.w)
        for b in writes:
            if b.w is not None:
                deps.add(b.w)
            deps.update(b.r.values())
            deps.update(b.rd)
        for b in reads:
            if op.isdma:
                b.rd.append(op)
            else:
                b.r[op.eng] = op
        for b in writes:
            b.w = op
            b.r = {}
            b.rd = []
        ph = self.phase
        op.deps = [d for d in deps if d is not op and d.phase == ph and
                   not (d.eng == "pe" and op.eng == "pe" and not d.isdma and not op.isdma)]
        self.ops[op.eng].append(op)
        self.nops += 1

    def op(self, eng, fn, reads=(), writes=()):
        o = Op()
        o.eng = eng
        o.fn = fn
        o.needed = False
        o.ms = 0
        o.isdma = False
        o.phase = self.phase
        self._add(o, reads, writes)
        return o

    def dma(self, eng, out, in_, reads=(), writes=()):
        o = Op()
        o.eng = eng
        o.fn = lambda e: e.dma_start(out=out, in_=in_)
        o.needed = False
        o.ms = 0
        o.isdma = True
        o.phase = self.phase
        j = self.drr
        self.drr = (self.drr + 1) % len(self.dsems)
        self.dval[j] += 16
        o.dsem = self.dsems[j]
        o.dkey = "d%d" % j
        o.dval = self.dval[j]
        prev = self.dlast[j]
        self.dlast[j] = o
        self._add(o, reads, writes)
        if prev is not None and prev.phase == self.phase and prev not in o.deps:
            o.deps.append(prev)
        return o

    def flush(self):
        for e in self.ALLE:
            for op in self.ops[e]:
                for d in op.deps:
                    if not d.isdma:
                        d.needed = True
        for e in self.CE:
            for op in self.ops[e]:
                if (not op.isdma) and op.needed:
                    self.cnt[e] += 1
                    op.ms = self.cnt[e]
        sched = self

        def mk(e):
            def body(eng):
                wt = sched.waited[e]
                for op in sched.ops[e]:
                    need = {}
                    for d in op.deps:
                        if d.isdma:
                            k, s, v = d.dkey, d.dsem, d.dval
                        else:
                            k, s, v = d.eng, sched.sem[d.eng], d.ms
                        if k not in need or need[k][1] < v:
                            need[k] = (s, v)
                    for k, (s, v) in need.items():
                        if wt.get(k, 0) < v:
                            eng.wait_ge(s, v)
                            wt[k] = v
                    ins = op.fn(eng)
                    if op.isdma:
                        ins.then_inc(op.dsem, 16)
                    elif op.needed:
                        ins.then_inc(sched.sem[e], 1)
                if e == "sp":
                    for j, s in enumerate(sched.dsems):
                        k = "d%d" % j
                        if wt.get(k, 0) < sched.dval[j]:
                            eng.wait_ge(s, sched.dval[j])
                            wt[k] = sched.dval[j]
            return body

        with self.nc.Block() as block:
            block.tensor(mk("pe"))
            block.scalar(mk("act"))
            block.vector(mk("dve"))
            block.gpsimd(mk("pool"))
            block.sync(mk("sp"))
        self.ops = {e: [] for e in self.ALLE}
        self.phase += 1


def _bf16_round(a):
    a = np.asarray(a, np.float32)
    u = a.view(np.uint32)
    r = ((u >> 16) & 1) + 0x7FFF
    return ((u + r) & 0xFFFF0000).astype(np.uint32).view(np.float32)


def make_consts(P):
    c = {}
    c["ident"] = np.eye(128, dtype=np.float32)
    j = np.arange(128, dtype=np.float64)
    gam = 1.0 - 2.0 ** (-5.0 - np.arange(4, dtype=np.float64))
    lg = np.log(gam)
    dec = np.zeros((128, 4, 128), np.float64)
    for h in range(4):
        rel = j[None, :] - j[:, None]
        dec[:, h, :] = np.where(rel >= 0, np.exp(lg[h] * np.maximum(rel, 0)), 0.0)
    c["decT"] = dec.astype(np.float32)
    qd = np.zeros((128, 4, 512), np.float64)
    for h in range(4):
        qd[:, h, :] = np.exp(lg[h] * ((np.arange(512) % 128) + 1.0))[None, :]
    c["qdec"] = qd.astype(np.float32)
    kd = np.zeros((128, 512), np.float64)
    for h in range(4):
        kd[:, h * 128:(h + 1) * 128] = (np.exp(lg[h] * (127.0 - j)) * 128.0 ** -0.5)[:, None]
    c["kdec"] = kd.astype(np.float32)
    c["cdec"] = [float(np.exp(lg[h] * 128.0)) for h in range(4)]
    c["tri"] = (j[:, None] <= j[None, :]).astype(np.float32)
    c["tris"] = (j[:, None] < j[None, :]).astype(np.float32)
    slopes = 2.0 ** (-8.0 * (np.arange(4) + 1.0) / 4.0)
    scale = 64.0 ** -0.5
    tl = np.arange(512)
    aq = np.zeros((128, 4, 512), np.float32)
    for h in range(4):
        aq[64, h, :] = -(slopes[h] / scale) * (tl - (tl % 2))
        aq[65, h, :] = -(slopes[h] / scale) * (tl % 2)
        aq[66, h, :] = 1.0
        aq[67, h, :] = slopes[h] / scale
    c["aq"] = aq
    kx = np.zeros((128, P), np.float32)
    kx[64, :] = 1.0
    kx[65, :] = 1.0
    kx[66, :128 - NMETA] = NEG
    kx[67, :] = 128.0 * ((np.arange(P) // 128) % 2)
    c["kx"] = kx
    bt = np.zeros((128, 4, 72), np.float32)
    for h in range(4):
        for dl in range(-64, 8):
            bt[:, h, dl + 64] = slopes[h] * (128.0 * dl + j)
    c["btab"] = bt
    sbs = 128.0 ** -0.5
    ninv = -float(_bf16_round(np.float32(1.0 / sbs)))
    c["negU"] = (ninv * (j[:, None] >= j[None, :])).astype(np.float32)
    c["negOnes"] = np.full((128, 128), ninv, np.float32)
    c["sb_fix"] = float(sbs)
    pb = np.zeros((128, 1), np.float32)
    pb[:128 - NMETA] = NEG
    c["padb"] = pb
    return c


def build(G, depth=2, debug=False):
    NT = 1 + 4 * G
    P = 128 * NT
    groups = [(0, 1)] + [(1 + 4 * g, 4) for g in range(G)]
    nc = bass.Bass("TRN2", target_bir_lowering=False)
    es = contextlib.ExitStack()

    def din(name, shape, dt=F32):
        return nc.dram_tensor(name, list(shape), dt, kind="ExternalInput").ap()

    def dscr(name, shape, dt):
        kind = "ExternalOutput" if debug else "Internal"
        return nc.dram_tensor(name, list(shape), dt, kind=kind).ap()

    h0 = din("h0", [P, D])
    w_in0 = din("w_in0", [D, 3584])
    w_out0 = din("w_out0", [D, D])
    w_in1 = din("w_in1", [D, 3072])
    w_out1 = din("w_out1", [D, D])
    ups = [din("up%d" % i, [D, 2 * DFF]) for i in range(2)]
    dns = [din("dn%d" % i, [DFF, D]) for i in range(2)]
    gains = din("gains", [128, 32])
    convw = din("convw", [128, 2, 22, 4])
    gret = din("gret", [128, 512])
    gdif = din("gdif", [128, 512])
    gfin = din("gfin", [128, D])
    lamv = din("lamv", [128, 4, 64])
    c_ident = din("c_ident", [128, 128])
    c_decT = din("c_decT", [128, 4, 128])
    c_qdec = din("c_qdec", [128, 4, 512])
    c_kdec = din("c_kdec", [128, 512])
    c_tri = din("c_tri", [128, 128])
    c_tris = din("c_tris", [128, 128])
    c_aq = din("c_aq", [128, 4, 512])
    c_kx = din("c_kx", [128, P])
    c_btab = din("c_btab", [128, 4, 72])
    c_negU = din("c_negU", [128, 128])
    c_negOnes = din("c_negOnes", [128, 128])
    c_padb = din("c_padb", [128, 1])
    out = nc.dram_tensor("out", [P - 128, D], F32, kind="ExternalOutput").ap()

    Hs = dscr("Hs", [P, D], F32)
    Wb_in0 = dscr("Wb_in0", [D, 3584], BF16)
    Wb_out0 = dscr("Wb_out0", [D, D], BF16)
    Wb_in1 = dscr("Wb_in1", [D, 3072], BF16)
    Wb_out1 = dscr("Wb_out1", [D, D], BF16)
    Wb_up = [dscr("Wb_up%d" % i, [D, 2 * DFF], BF16) for i in range(2)]
    Wb_dn = [dscr("Wb_dn%d" % i, [DFF, D], BF16) for i in range(2)]
    FM = dscr("FM", [16, 128, P], BF16)
    TM = dscr("TM", [P, 2048], BF16)
    MIX = dscr("MIX", [P, D], BF16)
    S = Sched(nc, es)
    consts = make_consts(P)
    cdec = consts["cdec"]
    sb_scale = consts["sb_fix"]

    def rr(k):
        return ("act", "dve")[k % 2]

    def SBT(name, shape, dt):
        return nc.sbuf_tensor("%s_p%d" % (name, S.phase), shape, dt)

    def PST(name, shape, dt):
        return nc.psum_tensor("%s_p%d" % (name, S.phase), shape, dt)

    def evac(eng, out_ap, in_ap, reads, writes, scale=None):
        if eng == "act":
            if scale is None:
                S.op("act", lambda e: e.activation(out=out_ap, in_=in_ap, func=AF.Copy), reads, writes)
            else:
                S.op("act", lambda e: e.activation(out=out_ap, in_=in_ap, func=AF.Copy, scale=float(scale)),
                     reads, writes)
        else:
            if scale is None:
                S.op(eng, lambda e: e.tensor_copy(out=out_ap, in_=in_ap), reads, writes)
            else:
                S.op(eng, lambda e: e.tensor_scalar(out=out_ap, in0=in_ap, scalar1=float(scale), scalar2=None,
                                                    op0=ALU.mult), reads, writes)

    def load_const_bf16(pes, name, src, shape, stage, stageB):
        t = pes.enter_context(SBT(name, shape, BF16))
        b = Buf()
        n = int(np.prod(shape[1:]))
        sv = stage[:, 0:n]
        S.dma("sp", sv, src if len(shape) == 2 else src.rearrange("p a b -> p (a b)"), writes=[stageB])
        tv = t[:] if len(shape) == 2 else t[:].rearrange("p a b -> p (a b)")
        S.op("dve", lambda e: e.tensor_copy(out=tv, in_=sv), [stageB], [b])
        return t, b

    def load_const_f32(pes, name, src, shape):
        t = pes.enter_context(SBT(name, shape, F32))
        b = Buf()
        S.dma("sp", t[:], src, writes=[b])
        return t, b

    def wjob_blocks(jobs):
        for (src, dst, gi, nrc, C, nsp) in jobs:
            cw = C // nsp
            for rc in range(nrc):
                for cs in range(nsp):
                    yield (src[rc * 128:(rc + 1) * 128, cs * cw:(cs + 1) * cw],
                           dst[rc * 128:(rc + 1) * 128, cs * cw:(cs + 1) * cw],
                           None if gi is None else gi * 8 + rc, cw)

    jobs_early = [(w_in0, Wb_in0, 0, 8, 3584, 2)]
    jobs_late = [(w_out0, Wb_out0, None, 8, D, 1), (ups[0], Wb_up[0], 2, 8, 2 * DFF, 2),
                 (dns[0], Wb_dn[0], None, 22, D, 1)]
    if depth > 1:
        jobs_late += [(w_in1, Wb_in1, 1, 8, 3072, 2), (w_out1, Wb_out1, None, 8, D, 1),
                      (ups[1], Wb_up[1], 3, 8, 2 * DFF, 2), (dns[1], Wb_dn[1], None, 22, D, 1)]

    class WConv:
        def __init__(self, pes, nbuf, engs):
            CW = 2816
            self.n = nbuf
            self.stg = [pes.enter_context(SBT("wst%d" % i, [128, CW], F32)) for i in range(nbuf)]
            self.ob = [pes.enter_context(SBT("wob%d" % i, [128, CW], BF16)) for i in range(nbuf)]
            self.stgB = [Buf() for _ in range(nbuf)]
            self.obB = [Buf() for _ in range(nbuf)]
            self.gt, self.gB = load_const_f32(pes, "gains_t", gains, [128, 32])
            self.engs = engs
            self.k = 0

        def load(self, blk):
            i = self.k % self.n
            src, dst, gcol, cw = blk
            S.dma("act", self.stg[i][:, 0:cw], src, writes=[self.stgB[i]])
            self.k += 1
            return (i, blk)

        def convert_store(self, tok):
            i, (src, dst, gcol, cw) = tok
            sv = self.stg[i][:, 0:cw]
            ov = self.ob[i][:, 0:cw]
            eng = self.engs[i % len(self.engs)]
            if gcol is None:
                evac(eng, ov, sv, [self.stgB[i]], [self.obB[i]])
            else:
                gv = self.gt[:, gcol:gcol + 1]
                if eng == "act":
                    S.op("act", lambda e: e.activation(out=ov, in_=sv, func=AF.Copy, scale=gv),
                         [self.stgB[i], self.gB], [self.obB[i]])
                else:
                    S.op("dve", lambda e: e.tensor_scalar(out=ov, in0=sv, scalar1=gv, scalar2=None, op0=ALU.mult),
                         [self.stgB[i], self.gB], [self.obB[i]])
            S.dma("sp", dst, ov, reads=[self.obB[i]])

    def phase_wprep():
        with contextlib.ExitStack() as pes:
            wc = WConv(pes, 4, ("act", "dve"))
            toks = []
            for blk in wjob_blocks(jobs_early):
                toks.append(wc.load(blk))
                if len(toks) > 2:
                    wc.convert_store(toks.pop(0))
            while toks:
                wc.convert_store(toks.pop(0))
            S.flush()

    class NormT:
        def __init__(self, pes, nyT=1, nhb=2, nyb=2):
            self.hb = [pes.enter_context(SBT("n_hb%d" % i, [128, D], F32)) for i in range(nhb)]
            self.hbB = [Buf() for _ in range(nhb)]
            self.yb = [pes.enter_context(SBT("n_yb%d" % i, [128, D], BF16)) for i in range(nyb)]
            self.ybB = [Buf() for _ in range(nyb)]
            self.yT = [pes.enter_context(SBT("n_yT%d" % i, [128, 8, 512], BF16)) for i in range(nyT)]
            self.yTB = [Buf() for _ in range(nyT)]
            self.ss = [pes.enter_context(SBT("n_ss%d" % i, [128, 4], F32)) for i in range(2)]
            self.ssB = [Buf() for _ in range(2)]
            self.pT = [pes.enter_context(PST("n_pT%d" % i, [128, 8, 128], BF16)) for i in range(2)]
            self.pTB = [Buf() for _ in range(2)]
            self.k = 0
            self.ky = 0
            self.g = 0
            self.kp = 0

    def norm_a(R, src, t0, nt):
        gi = R.g
        R.g += 1
        ss = R.ss[gi % 2]
        ssB = R.ssB[gi % 2]
        nhb = len(R.hb)
        res = []
        for i in range(nt):
            s = R.k % nhb
            R.k += 1
            ys = R.ky % len(R.yb)
            R.ky += 1
            hb, hbB, yb, ybB = R.hb[s], R.hbB[s], R.yb[ys], R.ybB[ys]
            tile = t0 + i
            S.dma("act", hb[:], src[tile * 128:(tile + 1) * 128, :], writes=[hbB])
            ssc = ss[:, i:i + 1]
            S.op("act", lambda e, yb=yb, hb=hb, ssc=ssc: e.activation(out=yb[:], in_=hb[:], func=AF.Square,
                                                                      accum_out=ssc), [hbB], [ybB, ssB])
            S.op("dve", lambda e, ssc=ssc: e.tensor_scalar(out=ssc, in0=ssc, scalar1=1.0 / D, scalar2=EPS,
                                                           op0=ALU.mult, op1=ALU.add), [ssB], [ssB])
            S.op("act", lambda e, ssc=ssc: e.activation(out=ssc, in_=ssc, func=AF.Sqrt), [ssB], [ssB])
            S.op("dve", lambda e, ssc=ssc: e.reciprocal(out=ssc, in_=ssc), [ssB], [ssB])
            S.op("act", lambda e, yb=yb, hb=hb, ssc=ssc: e.activation(out=yb[:], in_=hb[:], func=AF.Copy, scale=ssc),
                 [hbB, ssB], [ybB])
            res.append((yb, ybB))
        return res

    def na_load(R, src, tile):
        s_ = R.k % len(R.hb)
        R.k += 1
        S.dma("act", R.hb[s_][:], src[tile * 128:(tile + 1) * 128, :], writes=[R.hbB[s_]])
        return s_

    def na_compute(R, s_, gpar, i):
        ss = R.ss[gpar % 2]
        ssB = R.ssB[gpar % 2]
        ys = R.ky % len(R.yb)
        R.ky += 1
        hb, hbB, yb, ybB = R.hb[s_], R.hbB[s_], R.yb[ys], R.ybB[ys]
        ssc = ss[:, i:i + 1]
        S.op("act", lambda e, yb=yb, hb=hb, ssc=ssc: e.activation(out=yb[:], in_=hb[:], func=AF.Square,
                                                                  accum_out=ssc), [hbB], [ybB, ssB])
        S.op("dve", lambda e, ssc=ssc: e.tensor_scalar(out=ssc, in0=ssc, scalar1=1.0 / D, scalar2=EPS,
                                                       op0=ALU.mult, op1=ALU.add), [ssB], [ssB])
        S.op("act", lambda e, ssc=ssc: e.activation(out=ssc, in_=ssc, func=AF.Sqrt), [ssB], [ssB])
        S.op("dve", lambda e, ssc=ssc: e.reciprocal(out=ssc, in_=ssc), [ssB], [ssB])
        S.op("act", lambda e, yb=yb, hb=hb, ssc=ssc: e.activation(out=yb[:], in_=hb[:], func=AF.Copy, scale=ssc),
             [hbB, ssB], [ybB])
        return (yb, ybB)

    def norm_b(R, ident, identB, ybs):
        gi = R.kp
        R.kp += 1
        yT = R.yT[gi % len(R.yT)]
        yTB = R.yTB[gi % len(R.yT)]
        for i, (yb, ybB) in enumerate(ybs):
            ps = (gi * 4 + i) % 2
            pT, pTB = R.pT[ps], R.pTB[ps]
            for c in range(8):
                S.op("pe", lambda e, pT=pT, yb=yb, c=c: e.transpose(out=pT[:, c, :], in_=yb[:, c * 128:(c + 1) * 128],
                                                                   identity=ident[:]), [ybB, identB], [pTB])
            S.op("dve", lambda e, yT=yT, pT=pT, i=i: e.tensor_copy(out=yT[:, :, i * 128:(i + 1) * 128], in_=pT[:]),
                 [pTB], [yTB])
        return yT, yTB

    def norm_b_tile(R, ident, identB, yb, ybB, i):
        gi = R.kp - 1
        yT = R.yT[gi % len(R.yT)]
        yTB = R.yTB[gi % len(R.yT)]
        ps = (gi * 4 + i) % 2
        pT, pTB = R.pT[ps], R.pTB[ps]
        for c in range(8):
            S.op("pe", lambda e, pT=pT, yb=yb, c=c: e.transpose(out=pT[:, c, :], in_=yb[:, c * 128:(c + 1) * 128],
                                                               identity=ident[:]), [ybB, identB], [pTB])
        S.op("dve", lambda e, yT=yT, pT=pT, i=i: e.tensor_copy(out=yT[:, :, i * 128:(i + 1) * 128], in_=pT[:]),
             [pTB], [yTB])
        return yT, yTB

    def norm_transpose(R, ident, identB, src, t0, nt):
        gi = R.kp
        R.kp += 1
        slots = {0: na_load(R, src, t0)}
        yT = yTB = None
        for i in range(nt):
            if i + 1 < nt:
                slots[i + 1] = na_load(R, src, t0 + i + 1)
            yb, ybB = na_compute(R, slots[i], gi, i)
            yT, yTB = norm_b_tile(R, ident, identB, yb, ybB, i)
        return yT, yTB

    def phase_proj(layer, src):
        if layer == 0:
            Wb, C = Wb_in0, 3584
            kscl = 128.0 ** -0.5
            fm = [(c0, None) for c0 in range(0, 512, 128)] + [(c0, kscl) for c0 in range(512, 1024, 128)] + \
                 [(c0, None) for c0 in range(2048, 3072, 128)]
            tm = [512, 1024, 1536, 3072]
        else:
            Wb, C = Wb_in1, 3072
            fm = [(c0, None) for c0 in range(0, 2048, 128)]
            tm = [2048, 2560]
        nF = len(fm)
        nTM = len(tm)
        with contextlib.ExitStack() as pes:
            W = pes.enter_context(SBT("pj_W", [128, 8, C], BF16))
            WB = Buf()
            for kc in range(8):
                S.dma("sp", W[:, kc, :], Wb[kc * 128:(kc + 1) * 128, :], writes=[WB])
            stage = pes.enter_context(SBT("pj_stage", [128, 128], F32))
            stageB = Buf()
            ident, identB = load_const_bf16(pes, "pj_ident", c_ident, [128, 128], stage, stageB)
            R = NormT(pes, nyT=2, nhb=2, nyb=4)
            FT = [pes.enter_context(SBT("pj_FT%d" % i, [128, nF, 512], BF16)) for i in range(2)]
            FTB = [[Buf() for _ in range(nF)] for _ in range(2)]
            TT = [pes.enter_context(SBT("pj_TT%d" % i, [128, nTM * 512], BF16)) for i in range(2)]
            TTB = [[Buf() for _ in range(nTM)] for _ in range(2)]
            pF = [pes.enter_context(PST("pj_pF%d" % i, [128, 512], F32)) for i in range(3)]
            pFB = [Buf() for _ in range(3)]
            k = 0
            kt = 0
            yT, yTB = norm_transpose(R, ident, identB, src, groups[0][0], groups[0][1])
            for gi, (t0, nt) in enumerate(groups):
                N = 128 * nt
                ft, ftB = FT[gi % 2], FTB[gi % 2]
                for f, (c0, scl) in enumerate(fm):
                    p, pB = pF[k % 3], pFB[k % 3]
                    for kc in range(8):
                        S.op("pe", lambda e, p=p, kc=kc, c0=c0, yT=yT, N=N: e.matmul(
                            p[:, 0:N], lhsT=W[:, kc, c0:c0 + 128], rhs=yT[:, kc, 0:N], start=(kc == 0),
                            stop=(kc == 7)), [WB, yTB], [pB])
                    evac(rr(k), ft[:, f, 0:N], p[:, 0:N], [pB], [ftB[f]], scale=scl)
                    k += 1
                S.dma("sp", FM[0:nF, :, t0 * 128:t0 * 128 + N].rearrange("f p n -> p f n"), ft[:, :, 0:N],
                      reads=ftB)
                have_next = gi + 1 < len(groups)
                ybn = []
                if have_next:
                    t0n, ntn = groups[gi + 1]
                    R.kp += 1
                    nsl = {0: na_load(R, src, t0n)}
                    for j in range(ntn):
                        if j + 1 < ntn:
                            nsl[j + 1] = na_load(R, src, t0n + j + 1)
                        ybn.append(na_compute(R, nsl[j], gi + 1, j))
                for i in range(nt):
                    tt, ttB = TT[kt % 2], TTB[kt % 2]
                    kt += 1
                    for j, c0 in enumerate(tm):
                        p, pB = pF[k % 3], pFB[k % 3]
                        for kc in range(8):
                            S.op("pe", lambda e, p=p, kc=kc, c0=c0, yT=yT, i=i: e.matmul(
                                p[:], lhsT=yT[:, kc, i * 128:(i + 1) * 128], rhs=W[:, kc, c0:c0 + 512],
                                start=(kc == 0), stop=(kc == 7)), [WB, yTB], [pB])
                        evac(rr(k), tt[:, j * 512:(j + 1) * 512], p[:], [pB], [ttB[j]])
                        k += 1
                    tile = t0 + i
                    S.dma("sp", TM[tile * 128:(tile + 1) * 128, 0:nTM * 512], tt[:], reads=ttB)
                if have_next:
                    for j, (yb_, ybB_) in enumerate(ybn):
                        yTn, yTnB = norm_b_tile(R, ident, identB, yb_, ybB_, j)
                    yT, yTB = yTn, yTnB
            S.flush()

    def phase_retention():
        with contextlib.ExitStack() as pes:
            decT, decTB = load_const_f32(pes, "rt_decT", c_decT, [128, 4, 128])
            qdec, qdecB = load_const_f32(pes, "rt_qdec", c_qdec, [128, 4, 512])
            kdec, kdecB = load_const_f32(pes, "rt_kdec", c_kdec, [128, 512])
            gr, grB = load_const_f32(pes, "rt_gret", gret, [128, 512])
            QT = [pes.enter_context(SBT("rt_QT%d" % i, [128, 4, 512], BF16)) for i in range(2)]
            KT = [pes.enter_context(SBT("rt_KT%d" % i, [128, 4, 512], BF16)) for i in range(2)]
            Qd = [pes.enter_context(SBT("rt_Qd%d" % i, [128, 4, 512], BF16)) for i in range(2)]
            TMg = [pes.enter_context(SBT("rt_TM%d" % i, [128, 4, 1536], BF16)) for i in range(2)]
            Kd = [pes.enter_context(SBT("rt_Kd%d" % i, [128, 512], BF16)) for i in range(2)]
            QTB, KTB, QdB, TMB, KdB = ([Buf() for _ in range(2)] for _ in range(5))
            S32 = pes.enter_context(SBT("rt_S32", [128, 4, 128], F32))
            Sb = pes.enter_context(SBT("rt_Sb", [128, 4, 128], BF16))
            S32B, SbB = Buf(), Buf()
            sTm = [pes.enter_context(SBT("rt_sTm%d" % i, [128, 4, 128], BF16)) for i in range(2)]
            sTmB = [Buf() for _ in range(2)]
            sq = pes.enter_context(SBT("rt_sq", [128, 4, 128], F32))
            sqB = Buf()
            st = [pes.enter_context(SBT("rt_st%d" % i, [128, 16], F32)) for i in range(3)]
            stB = [Buf() for _ in range(3)]
            Yr = [pes.enter_context(SBT("rt_Yr%d" % i, [128, 512], F32)) for i in range(2)]
            YrB = [Buf() for _ in range(2)]
            sg = [pes.enter_context(SBT("rt_sg%d" % i, [128, 512], F32)) for i in range(2)]
            sgB = [Buf() for _ in range(2)]
            mo = [pes.enter_context(SBT("rt_mo%d" % i, [128, 512], BF16)) for i in range(2)]
            moB = [Buf() for _ in range(2)]
            psT = [pes.enter_context(PST("rt_psT%d" % i, [128, 4, 128], F32)) for i in range(2)]
            po = [pes.enter_context(PST("rt_po%d" % i, [128, 4, 128], F32)) for i in range(3)]
            pkv = [pes.enter_context(PST("rt_pkv%d" % i, [128, 4, 128], F32)) for i in range(2)]
            psTB, poB, pkvB = ([Buf() for _ in range(3)] for _ in range(3))
            S.op("dve", lambda e: e.memset(S32[:], 0.0), [], [S32B])
            S.op("dve", lambda e: e.memset(Sb[:], 0.0), [], [SbB])
            def gn1_stage(tile, b, i, tb, t3):
                s_ = st[t3]
                sB_ = stB[t3]
                S.op("dve", lambda e, s_=s_, t3=t3: e.tensor_reduce(out=s_[:, 0:4], in_=po[t3][:], axis=AX.X,
                                                                    op=ALU.add), [poB[t3]], [sB_])
                S.op("act", lambda e, t3=t3: e.activation(out=sq[:], in_=po[t3][:], func=AF.Square),
                     [poB[t3]], [sqB])
                S.op("dve", lambda e, s_=s_: e.tensor_reduce(out=s_[:, 4:8], in_=sq[:], axis=AX.X, op=ALU.add),
                     [sqB], [sB_])
                S.op("dve", lambda e, s_=s_: e.tensor_scalar(out=s_[:, 0:4], in0=s_[:, 0:4], scalar1=1.0 / 128,
                                                             scalar2=None, op0=ALU.mult), [sB_], [sB_])
                S.op("dve", lambda e, s_=s_: e.tensor_tensor(out=s_[:, 8:12], in0=s_[:, 0:4], in1=s_[:, 0:4],
                                                             op=ALU.mult), [sB_], [sB_])
                S.op("dve", lambda e, s_=s_: e.scalar_tensor_tensor(
                    out=s_[:, 4:8], in0=s_[:, 4:8], scalar=1.0 / 128, in1=s_[:, 8:12], op0=ALU.mult,
                    op1=ALU.subtract), [sB_], [sB_])
                S.op("dve", lambda e, s_=s_: e.tensor_scalar(out=s_[:, 4:8], in0=s_[:, 4:8], scalar1=EPS,
                                                             scalar2=None, op0=ALU.add), [sB_], [sB_])
                S.op("act", lambda e, s_=s_: e.activation(out=s_[:, 4:8], in_=s_[:, 4:8], func=AF.Sqrt),
                     [sB_], [sB_])

            def gn2_stage(tile, b, i, tb, t3):
                s_ = st[t3]
                sB_ = stB[t3]
                S.op("dve", lambda e, s_=s_: e.reciprocal(out=s_[:, 4:8], in_=s_[:, 4:8]), [sB_], [sB_])
                yr, yrB = Yr[tb], YrB[tb]
                for h in range(4):
                    S.op("dve", lambda e, h=h, yr=yr, s_=s_, tb=tb: e.tensor_scalar(
                        out=yr[:, h * 128:(h + 1) * 128], in0=po[t3][:, h, :], scalar1=s_[:, h:h + 1],
                        scalar2=s_[:, 4 + h:5 + h], op0=ALU.subtract, op1=ALU.mult), [poB[t3], sB_], [yrB])
                S.op("pool", lambda e, yr=yr: e.tensor_tensor(out=yr[:], in0=yr[:], in1=gr[:], op=ALU.mult),
                     [yrB, grB], [yrB])
                S.op("act", lambda e, b=b, i=i, tb=tb: e.activation(out=sg[tb][:], in_=TMg[b][:, i, 1024:1536],
                                                                   func=AF.Silu), [TMB[b]], [sgB[tb]])
                S.op("pool", lambda e, yr=yr, tb=tb: e.tensor_tensor(out=mo[tb][:], in0=yr[:], in1=sg[tb][:],
                                                                     op=ALU.mult), [yrB, sgB[tb]], [moB[tb]])
                S.dma("sp", MIX[tile * 128:(tile + 1) * 128, 0:512], mo[tb][:], reads=[moB[tb]])

            gn_pending = None
            gn_pending2 = None
            tk = 0
            for gi, (t0, nt) in enumerate(groups):
                N = 128 * nt
                b = gi % 2
                S.dma("act", QT[b][:, :, 0:N], FM[0:4, :, t0 * 128:t0 * 128 + N].rearrange("f p n -> p f n"),
                      writes=[QTB[b]])
                S.dma("act", KT[b][:, :, 0:N], FM[4:8, :, t0 * 128:t0 * 128 + N].rearrange("f p n -> p f n"),
                      writes=[KTB[b]])
                S.dma("act", TMg[b][:, 0:nt, :],
                      TM[t0 * 128:t0 * 128 + N, 0:1536].rearrange("(i p) c -> p i c", p=128), writes=[TMB[b]])
                S.op("pool", lambda e, b=b, N=N: e.tensor_tensor(out=Qd[b][:, :, 0:N], in0=QT[b][:, :, 0:N],
                                                                 in1=qdec[:, :, 0:N], op=ALU.mult),
                     [QTB[b], qdecB], [QdB[b]])
                for i in range(nt):
                    tile = t0 + i
                    tb = tk % 2
                    t3 = tk % 3
                    tk += 1
                    cs = slice(i * 128, (i + 1) * 128)
                    S.op("pool", lambda e, b=b, i=i, tb=tb: e.tensor_tensor(out=Kd[tb][:], in0=TMg[b][:, i, 0:512],
                                                                           in1=kdec[:], op=ALU.mult),
                         [TMB[b], kdecB], [KdB[tb]])
                    for h in range(4):
                        S.op("pe", lambda e, b=b, h=h, cs=cs, tb=tb: e.matmul(
                            psT[tb][:, h, :], lhsT=KT[b][:, h, cs], rhs=QT[b][:, h, cs], start=True, stop=True),
                            [KTB[b], QTB[b]], [psTB[tb]])
                    S.op("dve", lambda e, tb=tb: e.tensor_tensor(out=sTm[tb][:], in0=psT[tb][:], in1=decT[:],
                                                                 op=ALU.mult), [psTB[tb], decTB], [sTmB[tb]])
                    for h in range(4):
                        S.op("pe", lambda e, b=b, h=h, i=i, tb=tb, t3=t3: e.matmul(
                            po[t3][:, h, :], lhsT=sTm[tb][:, h, :], rhs=TMg[b][:, i, 512 + h * 128:512 + (h + 1) * 128],
                            start=True, stop=False), [sTmB[tb], TMB[b]], [poB[t3]])
                        S.op("pe", lambda e, b=b, h=h, cs=cs, tb=tb, t3=t3: e.matmul(
                            po[t3][:, h, :], lhsT=Qd[b][:, h, cs], rhs=Sb[:, h, :], start=False, stop=True),
                            [QdB[b], SbB], [poB[t3]])
                    for h in range(4):
                        S.op("pe", lambda e, b=b, h=h, i=i, tb=tb: e.matmul(
                            pkv[tb][:, h, :], lhsT=Kd[tb][:, h * 128:(h + 1) * 128],
                            rhs=TMg[b][:, i, 512 + h * 128:512 + (h + 1) * 128], start=True, stop=True),
                            [KdB[tb], TMB[b]], [pkvB[tb]])
                    for h in range(4):
                        S.op("dve", lambda e, h=h, tb=tb: e.scalar_tensor_tensor(
                            out=S32[:, h, :], in0=S32[:, h, :], scalar=cdec[h], in1=pkv[tb][:, h, :],
                            op0=ALU.mult, op1=ALU.add), [S32B, pkvB[tb]], [S32B])
                    S.op("act", lambda e: e.activation(out=Sb[:], in_=S32[:], func=AF.Copy), [S32B], [SbB])
                    if gn_pending is not None:
                        gn1_stage(*gn_pending)
                    if gn_pending2 is not None:
                        gn2_stage(*gn_pending2)
                    gn_pending2 = gn_pending
                    gn_pending = (tile, b, i, tb, t3)
            if gn_pending is not None:
                gn1_stage(*gn_pending)
            if gn_pending2 is not None:
                gn2_stage(*gn_pending2)
            if gn_pending is not None:
                gn2_stage(*gn_pending)
            S.flush()

    def phase_diff():
        lam_init = 0.8 - 0.6 * math.exp(-0.3 * 0)
        scale = 64.0 ** -0.5
        with contextlib.ExitStack() as pes:
            stage = pes.enter_context(SBT("df_stage", [128, 2048], F32))
            stageB = Buf()
            tri, triB = load_const_bf16(pes, "df_tri", c_tri, [128, 128], stage, stageB)
            aq, aqB = load_const_bf16(pes, "df_aq", c_aq, [128, 4, 512], stage, stageB)
            btab, btabB = load_const_f32(pes, "df_btab", c_btab, [128, 4, 72])
            gd, gdB = load_const_f32(pes, "df_gd", gdif, [128, 512])
            lv, lvB = load_const_f32(pes, "df_lv", lamv, [128, 4, 64])
            lam = pes.enter_context(SBT("df_lam", [128, 8], F32))
            lamB = Buf()
            lj = pes.enter_context(SBT("df_lj", [128, 2, 64], F32))
            ljB = Buf()
            S.op("dve", lambda e: e.tensor_tensor(out=lj[:, 0, :], in0=lv[:, 0, :], in1=lv[:, 1, :], op=ALU.mult),
                 [lvB], [ljB])
            S.op("dve", lambda e: e.tensor_tensor(out=lj[:, 1, :], in0=lv[:, 2, :], in1=lv[:, 3, :], op=ALU.mult),
                 [lvB], [ljB])
            S.op("dve", lambda e: e.tensor_reduce(out=lam[:, 0:2], in_=lj[:], axis=AX.X, op=ALU.add), [ljB], [lamB])
            S.op("act", lambda e: e.activation(out=lam[:, 0:2], in_=lam[:, 0:2], func=AF.Exp), [lamB], [lamB])
            S.op("dve", lambda e: e.tensor_tensor(out=lam[:, 2:3], in0=lam[:, 1:2], in1=lam[:, 0:1],
                                                  op=ALU.subtract), [lamB], [lamB])
            S.op("dve", lambda e: e.tensor_scalar(out=lam[:, 3:4], in0=lam[:, 2:3], scalar1=-lam_init, scalar2=None,
                                                  op0=ALU.add), [lamB], [lamB])
            S.op("dve", lambda e: e.tensor_scalar(out=gd[:], in0=gd[:], scalar1=1.0 - lam_init, scalar2=None,
                                                  op0=ALU.mult), [gdB], [gdB])
            KTm2 = [[pes.enter_context(SBT("df_KT%d_%d" % (hp, m), [128, P], BF16)) for m in range(2)]
                    for hp in range(2)]
            KTm2B = [[Buf() for _ in range(2)] for _ in range(2)]
            Va2 = [pes.enter_context(SBT("df_Va%d" % hp, [128, NT, 132], BF16)) for hp in range(2)]
            Va2B = [Buf() for _ in range(2)]
            QTm = [[pes.enter_context(SBT("df_QT%d_%d" % (m, i), [128, 512], BF16)) for i in range(2)]
                   for m in range(2)]
            QTmB = [[Buf() for _ in range(2)] for _ in range(2)]
            pt = [pes.enter_context(SBT("df_pt%d" % i, [128, 2, 512], BF16)) for i in range(3)]
            ptB = [Buf() for _ in range(3)]
            O1 = pes.enter_context(SBT("df_O1", [128, 4, 128], F32))
            O1B = Buf()
            dif = [pes.enter_context(SBT("df_dif%d" % i, [128, 128], F32)) for i in range(2)]
            difB = [Buf() for _ in range(2)]
            dsq = pes.enter_context(SBT("df_dsq", [128, 128], F32))
            dsqB = Buf()
            rd = [pes.enter_context(SBT("df_rd%d" % i, [128, 4], F32)) for i in range(2)]
            rdB = [Buf() for _ in range(2)]
            mo = [pes.enter_context(SBT("df_mo%d" % i, [128, 128], BF16)) for i in range(2)]
            moB = [Buf() for _ in range(2)]
            psT = [pes.enter_context(PST("df_psT%d" % i, [128, 2, 512], F32)) for i in range(2)]
            psTB = [Buf() for _ in range(2)]
            acc = [[pes.enter_context(PST("df_acc%d_%d" % (a, j), [128, 2, 256], F32)) for j in range(2)]
                   for a in range(2)]
            accB = [[Buf() for _ in range(2)] for _ in range(2)]
            for hp in range(2):
                S.op("pool", lambda e, hp=hp: e.memset(Va2[hp][:], 1.0), [], [Va2B[hp]])
            for m in range(2):
                for i in range(2):
                    S.op("pool", lambda e, m=m, i=i: e.memset(QTm[m][i][:], 0.0), [], [QTmB[m][i]])
            for cc in range(0, P, 2048):
                ce = min(P, cc + 2048)
                S.dma("sp", stage[64:68, 0:ce - cc], c_kx[64:68, cc:ce], writes=[stageB])
                for hp in range(2):
                    for m in range(2):
                        S.op("pool", lambda e, m=m, hp=hp, cc=cc, ce=ce: e.tensor_copy(
                            out=KTm2[hp][m][64:68, cc:ce], in_=stage[64:68, 0:ce - cc]), [stageB], [KTm2B[hp][m]])
            wc = WConv(pes, 2, ("dve",))
            wblocks = wjob_blocks(jobs_late)
            wtoks = []

            def wstep():
                blk = next(wblocks, None)
                if blk is not None:
                    wtoks.append(wc.load(blk))
                if wtoks and (len(wtoks) > 1 or blk is None):
                    wc.convert_store(wtoks.pop(0))
                return blk is not None or bool(wtoks)
            segs = [(h, gi) for h in range(4) for gi in range(len(groups))]

            def load_kv(h):
                hp = h % 2
                for m in range(2):
                    S.dma("act", KTm2[hp][m][0:64, :], FM[12 + h, m * 64:(m + 1) * 64, :], writes=[KTm2B[hp][m]])
                    S.dma("act", KTm2[hp][m][68:128, :], FM[12 + h, (1 - m) * 64:(1 - m) * 64 + 60, :],
                          writes=[KTm2B[hp][m]])
                S.dma("act", Va2[hp][:, :, 0:128],
                      TM[:, 1536 + h * 128:1536 + (h + 1) * 128].rearrange("(i p) c -> p i c", p=128),
                      writes=[Va2B[hp]])

            def load_q(k):
                h, gi = segs[k]
                t0, nt = groups[gi]
                N = 128 * nt
                qb = k % 2
                for m in range(2):
                    S.dma("act", QTm[m][qb][0:64, 0:N], FM[8 + h, m * 64:(m + 1) * 64, t0 * 128:t0 * 128 + N],
                          writes=[QTmB[m][qb]])
                    if k < 2 or segs[k - 2][0] != h:
                        S.op("pool", lambda e, m=m, qb=qb, h=h: e.tensor_copy(out=QTm[m][qb][64:68, :],
                                                                              in_=aq[64:68, h, :]),
                             [aqB], [QTmB[m][qb]])

            load_kv(0)
            load_q(0)
            d4 = [pes.enter_context(SBT("df_d4_%d" % i, [128, 4, 128], F32)) for i in range(2)]
            d4B = [Buf() for _ in range(2)]
            dsq4 = pes.enter_context(SBT("df_dsq4", [128, 4, 128], F32))
            dsq4B = Buf()
            rr4 = [pes.enter_context(SBT("df_rr4_%d" % i, [128, 16], F32)) for i in range(2)]
            rr4B = [Buf() for _ in range(2)]
            mo4 = [pes.enter_context(SBT("df_mo4_%d" % i, [128, 4, 128], BF16)) for i in range(2)]
            mo4B = [Buf() for _ in range(2)]
            epsb = pes.enter_context(SBT("df_epsb", [128, 1], F32))
            epsbB = Buf()
            S.op("dve", lambda e: e.memset(epsb[:], EPS), [], [epsbB])
            fcount = [0]
            pending = None

            def make_final(h, t0, nt, m, a):
                def emit():
                    fp = fcount[0] % 2
                    fcount[0] += 1
                    rr, rrB = rr4[fp], rr4B[fp]
                    accs = [(acc[a][i // 2], accB[a][i // 2]) for i in range(nt)]
                    for i in range(nt):
                        ac, acB = accs[i]
                        S.op("dve", lambda e, ac=ac, i=i: e.tensor_scalar(
                            out=rr[:, i:i + 1], in0=ac[:, i % 2, 128:129], scalar1=1e-30, scalar2=None, op0=ALU.add),
                            [acB], [rrB])
                    S.op("dve", lambda e: e.reciprocal(out=rr[:, 0:nt], in_=rr[:, 0:nt]), [rrB], [rrB])
                    if m == 0:
                        for i in range(nt):
                            ac, acB = accs[i]
                            S.op("dve", lambda e, ac=ac, i=i: e.tensor_scalar(
                                out=O1[:, i, :], in0=ac[:, i % 2, 0:128], scalar1=rr[:, i:i + 1], scalar2=None,
                                op0=ALU.mult), [acB, rrB], [O1B])
                        return
                    d_, dB_ = d4[fp], d4B[fp]
                    for i in range(nt):
                        ac, acB = accs[i]
                        S.op("dve", lambda e, ac=ac, i=i: e.tensor_scalar(
                            out=d_[:, i, :], in0=ac[:, i % 2, 0:128], scalar1=rr[:, i:i + 1], scalar2=lam[:, 3:4],
                            op0=ALU.mult, op1=ALU.mult), [acB, rrB, lamB], [dB_])
                    S.op("dve", lambda e: e.tensor_tensor(out=d_[:, 0:nt, :], in0=d_[:, 0:nt, :], in1=O1[:, 0:nt, :],
                                                          op=ALU.add), [dB_, O1B], [dB_])
                    S.op("act", lambda e: e.activation(out=dsq4[:, 0:nt, :], in_=d_[:, 0:nt, :], func=AF.Square),
                         [dB_], [dsq4B])
                    S.op("dve", lambda e: e.tensor_reduce(out=rr[:, 4:4 + nt], in_=dsq4[:, 0:nt, :], axis=AX.X,
                                                          op=ALU.add), [dsq4B], [rrB])
                    S.op("act", lambda e: e.activation(out=rr[:, 8:8 + nt], in_=rr[:, 4:4 + nt], func=AF.Ln,
                                                       bias=epsb[:, 0:1], scale=1.0 / 128), [rrB, epsbB], [rrB])
                    S.op("act", lambda e: e.activation(out=rr[:, 8:8 + nt], in_=rr[:, 8:8 + nt], func=AF.Exp,
                                                       scale=-0.5), [rrB], [rrB])
                    mo_, moB_ = mo4[fp], mo4B[fp]
                    for i in range(nt):
                        S.op("dve", lambda e, i=i: e.scalar_tensor_tensor(
                            out=mo_[:, i, :], in0=d_[:, i, :], scalar=rr[:, 8 + i:9 + i],
                            in1=gd[:, h * 128:(h + 1) * 128], op0=ALU.mult, op1=ALU.mult), [dB_, rrB, gdB], [moB_])
                    S.dma("sp", MIX[t0 * 128:(t0 + nt) * 128, 512 + h * 128:512 + (h + 1) * 128].rearrange(
                        "(i p) c -> p i c", p=128), mo_[:, 0:nt, :], reads=[moB_])
                return emit

            gpend = []

            def dflush(keep):
                while len(gpend) > keep:
                    kbs2, rel2, c02, pp2, pp2B, a2, t02, nt2, Va_, VaB_, fin = gpend.pop(0)
                    for j, kb2 in enumerate(kbs2):
                        for i in range(max(rel2, 0), nt2):
                            ac, acB = acc[a2][i // 2], accB[a2][i // 2]
                            S.op("pe", lambda e, ac=ac, i=i, j=j, pp2=pp2, c02=c02, kb2=kb2, t02=t02, Va_=Va_:
                                 e.matmul(ac[:, i % 2, 0:129], lhsT=pp2[:, j, i * 128 - c02:(i + 1) * 128 - c02],
                                          rhs=Va_[:, kb2, 0:129], start=(kb2 == 0 and i % 2 == 0),
                                          stop=(kb2 == t02 + i), skip_group_check=True),
                                 [pp2B, VaB_], [acB])
                    if fin is not None:
                        finq.append([fin, 2])
                for ent in list(finq):
                    ent[1] -= 1
                    if ent[1] <= 0 or keep == 0:
                        ent[0]()
                        finq.remove(ent)

            finq = []

            fk = 0
            ak = 0
            for h in range(4):
                KTm, KTmB, Va, VaB = KTm2[h % 2], KTm2B[h % 2], Va2[h % 2], Va2B[h % 2]
                for gi, (t0, nt) in enumerate(groups):
                    N = 128 * nt
                    ksg = h * len(groups) + gi
                    qb = ksg % 2
                    if gi == 1 and h + 1 < 4:
                        load_kv(h + 1)
                    if ksg + 1 < len(segs):
                        load_q(ksg + 1)
                    for m in range(2):
                        a = ak % 2
                        ak += 1
                        nkb = t0 + nt
                        dunits = []
                        kb = 0
                        while kb < nkb:
                            if kb % 2 == 0 and kb + 1 < t0:
                                dunits.append([kb, kb + 1])
                                kb += 2
                            else:
                                dunits.append([kb])
                                kb += 1
                        nun = len(dunits)
                        for ui in range(nun):
                            if ui == min(2, nun - 1):
                                wstep()
                            kbs = dunits[ui]
                            nb = len(kbs)
                            rel = kbs[0] - t0
                            c0 = 128 * max(rel, 0)
                            ncol = N - c0
                            ps, psB = psT[fk % 2], psTB[fk % 2]
                            pp, ppB = pt[fk % 3], ptB[fk % 3]
                            fk += 1
                            for j, kb in enumerate(kbs):
                                S.op("pe", lambda e, ps=ps, m=m, kb=kb, j=j, qb=qb, c0=c0, N=N, ncol=ncol, KTm=KTm:
                                     e.matmul(ps[:, j, 0:ncol], lhsT=KTm[m][0:128, kb * 128:(kb + 1) * 128],
                                              rhs=QTm[m][qb][0:128, c0:N], start=True, stop=True),
                                     [KTmB[m], QTmB[m][qb]], [psB])
                            be = kbs[0] - (kbs[0] % 2) - t0
                            bv = btab[:, h, be + 64:be + 65]
                            S.op("act", lambda e, pp=pp, ps=ps, ncol=ncol, bv=bv, nb=nb: e.activation(
                                out=pp[:, 0:nb, 0:ncol], in_=ps[:, 0:nb, 0:ncol], func=AF.Exp, bias=bv,
                                scale=scale), [psB, btabB], [ppB])
                            if rel >= 0:
                                S.op("pool", lambda e, pp=pp: e.tensor_tensor(out=pp[:, 0, 0:128],
                                                                              in0=pp[:, 0, 0:128],
                                                                              in1=tri[:], op=ALU.mult),
                                     [ppB, triB], [ppB])
                            last = (ui == nun - 1)
                            gpend.append((kbs, rel, c0, pp, ppB, a, t0, nt, Va, VaB,
                                          make_final(h, t0, nt, m, a) if last else None))
                            dflush(2)
            dflush(0)
            while wstep():
                pass
            S.flush()

    def phase_sb():
        with contextlib.ExitStack() as pes:
            stage = pes.enter_context(SBT("sb_stage", [128, 128], F32))
            stageB = Buf()
            tris, trisB = load_const_bf16(pes, "sb_tris", c_tris, [128, 128], stage, stageB)
            negU, negUB = load_const_bf16(pes, "sb_negU", c_negU, [128, 128], stage, stageB)
            negO, negOB = load_const_bf16(pes, "sb_negO", c_negOnes, [128, 128], stage, stageB)
            padb, padbB = load_const_f32(pes, "sb_padb", c_padb, [128, 1])
            zb = pes.enter_context(SBT("sb_zb", [128, 1], F32))
            ob1 = pes.enter_context(SBT("sb_ob1", [128, 1], F32))
            zbB = Buf()
            S.op("dve", lambda e: e.memset(zb[:], 0.0), [], [zbB])
            S.op("dve", lambda e: e.memset(ob1[:], 1.0), [], [zbB])
            KT2 = [pes.enter_context(SBT("sb_KT%d" % i, [128, P], BF16)) for i in range(2)]
            KT2B = [Buf() for _ in range(2)]
            V2 = [pes.enter_context(SBT("sb_V%d" % i, [128, NT, 128], BF16)) for i in range(2)]
            V2B = [Buf() for _ in range(2)]
            QT = [pes.enter_context(SBT("sb_QT%d" % i, [128, 512], BF16)) for i in range(3)]
            QTB = [Buf() for _ in range(3)]
            A32 = [pes.enter_context(SBT("sb_A32_%d" % i, [128, 512], F32)) for i in range(2)]
            A16 = [[pes.enter_context(SBT("sb_A16_%d_%d" % (i, j), [128, 512], BF16)) for j in range(2)]
                   for i in range(2)]
            A32B = [Buf() for _ in range(2)]
            A16B = [[Buf() for _ in range(2)] for _ in range(2)]
            apar = [0, 0]
            NB = 3
            u = [pes.enter_context(SBT("sb_u%d" % i, [128, 2, 512], F32)) for i in range(2)]
            uB = [Buf() for _ in range(2)]
            sp = [pes.enter_context(SBT("sb_sp%d" % i, [128, 2, 512], BF16)) for i in range(NB)]
            spB = [Buf() for _ in range(NB)]
            w = [pes.enter_context(SBT("sb_w%d" % i, [128, 2, 512], BF16)) for i in range(NB)]
            wB = [Buf() for _ in range(NB)]
            mo = [pes.enter_context(SBT("sb_mo%d" % i, [128, 4, 128], BF16)) for i in range(2)]
            moB = [Buf() for _ in range(2)]
            pz = [pes.enter_context(PST("sb_pz%d" % i, [128, 2, 512], F32)) for i in range(1)]
            pzB = [Buf() for _ in range(1)]
            pE = [pes.enter_context(PST("sb_pE%d" % i, [128, 2, 512], F32)) for i in range(2)]
            pEB = [Buf() for _ in range(2)]
            acc = [pes.enter_context(PST("sb_acc%d" % i, [128, 4, 128], F32)) for i in range(2)]
            accB = [Buf() for _ in range(2)]
            units = []
            for h in range(8):
                for gi, (t0, nt) in enumerate(groups):
                    for kb in range(t0 + nt - 1, t0 - 1, -1):
                        units.append((h, gi, t0, nt, [kb]))
                    kb = t0 - 1
                    while kb >= 0:
                        if kb >= 2:
                            units.append((h, gi, t0, nt, [kb, kb - 1]))
                            kb -= 2
                        else:
                            units.append((h, gi, t0, nt, [kb]))
                            kb -= 1
            n = len(units)
            st = {}
            gcount = -1
            cur_h = -1
            cur_g = None
            ssegs = [(h, gi) for h in range(8) for gi in range(len(groups))]

            def sb_load_kv(h):
                S.dma("act", KT2[h % 2][:, :], FM[8 + h, :, :], writes=[KT2B[h % 2]])
                S.dma("act", V2[h % 2][:, :, :], TM[:, h * 128:(h + 1) * 128].rearrange("(i p) c -> p i c", p=128),
                      writes=[V2B[h % 2]])

            def sb_load_q(k):
                h, gi = ssegs[k]
                t0, nt = groups[gi]
                S.dma("act", QT[k % 3][:, 0:128 * nt], FM[h, :, t0 * 128:(t0 + nt) * 128], writes=[QTB[k % 3]])

            sb_load_kv(0)
            sb_load_q(0)

            def stage1(idx):
                nonlocal gcount, cur_h, cur_g
                h, gi, t0, nt, kbs = units[idx]
                N = 128 * nt
                nb = len(kbs)
                if (h, gi) != cur_g:
                    cur_g = (h, gi)
                    gcount += 1
                    g2 = gcount % 2
                    if gi == 1 and h + 1 < 8:
                        sb_load_kv(h + 1)
                    if gcount + 1 < len(ssegs):
                        sb_load_q(gcount + 1)
                    S.op("pool", lambda e, g2=g2: e.memset(A32[g2][:], 0.0), [], [A32B[g2]])
                    S.op("pool", lambda e, g2=g2: e.memset(A16[g2][0][:], 0.0), [], [A16B[g2][0]])
                    S.op("pool", lambda e, g2=g2: e.memset(A16[g2][1][:], 0.0), [], [A16B[g2][1]])
                g2 = gcount % 2
                q3 = gcount % 3
                rel = kbs[0] - t0
                c0 = 128 * max(rel, 0)
                ncol = N - c0
                first = (kbs[0] == t0 + nt - 1)
                z, zB = pz[0], pzB[0]
                uu, uuB = u[idx % 2], uB[idx % 2]
                s_, sB_ = sp[idx % NB], spB[idx % NB]
                bias = padb[:, 0:1] if kbs[-1] == 0 and nb == 1 else zb[:, 0:1]
                assert not (0 in kbs and nb == 2)
                for j, kb in enumerate(kbs):
                    S.op("pe", lambda e, j=j, kb=kb: e.matmul(z[:, j, 0:ncol], lhsT=KT2[h % 2][:, kb * 128:(kb + 1) * 128],
                                                              rhs=QT[q3][:, c0:N], start=True, stop=True),
                         [KT2B[h % 2], QTB[q3]], [zB])
                S.op("act", lambda e: e.activation(out=uu[:, 0:nb, 0:ncol], in_=z[:, 0:nb, 0:ncol], func=AF.Exp,
                                                   bias=bias, scale=sb_scale), [zB, padbB, zbB], [uuB])
                S.op("act", lambda e: e.activation(out=s_[:, 0:nb, 0:ncol], in_=uu[:, 0:nb, 0:ncol], func=AF.Ln,
                                                   bias=ob1[:, 0:1], scale=1.0), [uuB, zbB], [sB_])
                if rel >= 0:
                    S.op("pool", lambda e: e.tensor_tensor(out=s_[:, 0, 0:128], in0=s_[:, 0, 0:128], in1=tris[:],
                                                           op=ALU.mult), [sB_, trisB], [sB_])
                st[idx] = dict(q3=q3, g2=g2, rel=rel, c0=c0, ncol=ncol, first=first, N=N, bias=bias)

            def stage2(idx):
                h, gi, t0, nt, kbs = units[idx]
                nb = len(kbs)
                d = st[idx]
                q3 = d["q3"]
                g2, c0, ncol, N, first, rel, bias = d["g2"], d["c0"], d["ncol"], d["N"], d["first"], d["rel"], d["bias"]
                E, EB = pE[idx % 2], pEB[idx % 2]
                s_, sB_ = sp[idx % NB], spB[idx % NB]
                w_, wB_ = w[idx % NB], wB[idx % NB]
                for j, kb in enumerate(kbs):
                    S.op("pe", lambda e, j=j, kb=kb: e.matmul(E[:, j, 0:ncol], lhsT=KT2[h % 2][:, kb * 128:(kb + 1) * 128],
                                                              rhs=QT[q3][:, c0:N], start=True, stop=False),
                         [KT2B[h % 2], QTB[q3]], [EB])
                    S.op("pe", lambda e, j=j: e.matmul(E[:, j, 0:ncol], lhsT=negU[:], rhs=s_[:, j, 0:ncol], start=False,
                                                       stop=first), [negUB, sB_], [EB])
                    ap_ = apar[g2]
                    if not first:
                        S.op("pe", lambda e, j=j, ap_=ap_: e.matmul(E[:, j, 0:ncol], lhsT=negO[:],
                                                                    rhs=A16[g2][ap_][:, c0:N], start=False, stop=True),
                             [negOB, A16B[g2][ap_]], [EB])
                    if kb > 0:
                        S.op("dve", lambda e, j=j: e.tensor_tensor(out=A32[g2][:, c0:N], in0=A32[g2][:, c0:N],
                                                                   in1=s_[:, j, 0:ncol], op=ALU.add),
                             [A32B[g2], sB_], [A32B[g2]])
                        S.op("dve", lambda e, ap_=ap_: e.tensor_copy(out=A16[g2][1 - ap_][:, c0:N],
                                                                     in_=A32[g2][:, c0:N]),
                             [A32B[g2]], [A16B[g2][1 - ap_]])
                        apar[g2] = 1 - ap_
                S.op("act", lambda e: e.activation(out=w_[:, 0:nb, 0:ncol], in_=E[:, 0:nb, 0:ncol], func=AF.Exp,
                                                   bias=bias, scale=sb_scale), [EB, padbB, zbB], [wB_])
                if rel >= 0:
                    S.op("pool", lambda e: e.tensor_tensor(out=w_[:, 0, 0:128], in0=w_[:, 0, 0:128], in1=tris[:],
                                                           op=ALU.mult), [wB_, trisB], [wB_])

            def stage3(idx):
                h, gi, t0, nt, kbs = units[idx]
                d = st.pop(idx)
                g2, c0, rel = d["g2"], d["c0"], d["rel"]
                w_, wB_ = w[idx % NB], wB[idx % NB]
                ac, acB = acc[g2], accB[g2]
                for j, kb in enumerate(kbs):
                    for i in range(max(rel, 0), nt):
                        S.op("pe", lambda e, i=i, j=j, kb=kb: e.matmul(
                            ac[:, i, :], lhsT=w_[:, j, i * 128 - c0:(i + 1) * 128 - c0], rhs=V2[h % 2][:, kb, :],
                            start=(kb == t0 + nt - 1 and i == nt - 1), stop=(kb == 0), skip_group_check=True),
                            [wB_, V2B[h % 2]], [acB])
                if kbs[-1] == 0:
                    mo_, moB_ = mo[g2], moB[g2]
                    evac("dve", mo_[:, 0:nt, :], ac[:, 0:nt, :], [acB], [moB_])
                    S.dma("sp", MIX[t0 * 128:(t0 + nt) * 128, h * 128:(h + 1) * 128].rearrange(
                        "(i p) c -> p i c", p=128), mo_[:, 0:nt, :], reads=[moB_])

            for s in range(n + 2):
                if s < n:
                    stage1(s)
                if 0 <= s - 1 < n:
                    stage2(s - 1)
                if 0 <= s - 2 < n:
                    stage3(s - 2)
            S.flush()

    def phase_wout(layer, src, dst):
        Wb = Wb_out0 if layer == 0 else Wb_out1
        with contextlib.ExitStack() as pes:
            W = pes.enter_context(SBT("wo_W", [128, 8, D], BF16))
            WB = Buf()
            S.dma("sp", W[:], Wb.rearrange("(kc p) c -> p kc c", p=128), writes=[WB])
            stage = pes.enter_context(SBT("wo_stage", [128, 128], F32))
            stageB = Buf()
            ident, identB = load_const_bf16(pes, "wo_ident", c_ident, [128, 128], stage, stageB)
            mx = [pes.enter_context(SBT("wo_mx%d" % i, [128, D], BF16)) for i in range(3)]
            mxB = [Buf() for _ in range(3)]
            hb = [pes.enter_context(SBT("wo_hb%d" % i, [128, D], F32)) for i in range(3)]
            hbB = [Buf() for _ in range(3)]
            mT = [pes.enter_context(SBT("wo_mT%d" % i, [128, 8, 128], BF16)) for i in range(2)]
            mTB = [Buf() for _ in range(2)]
            pT = [pes.enter_context(PST("wo_pT%d" % i, [128, 8, 128], BF16)) for i in range(2)]
            pTB = [Buf() for _ in range(2)]
            pO = [pes.enter_context(PST("wo_pO%d" % i, [128, 512], F32)) for i in range(4)]
            pOB = [Buf() for _ in range(4)]
            def wo_front(tile):
                s3, s2 = tile % 3, tile % 2
                rows = slice(tile * 128, (tile + 1) * 128)
                S.dma("act", mx[s3][:], MIX[rows, :], writes=[mxB[s3]])
                S.dma("act", hb[s3][:], src[rows, :], writes=[hbB[s3]])
                for c in range(8):
                    S.op("pe", lambda e, c=c, s2=s2, s3=s3: e.transpose(out=pT[s2][:, c, :],
                                                                       in_=mx[s3][:, c * 128:(c + 1) * 128],
                                                                       identity=ident[:]), [mxB[s3], identB], [pTB[s2]])
                evac(rr(tile), mT[s2][:], pT[s2][:], [pTB[s2]], [mTB[s2]])

            wo_front(0)
            for tile in range(NT):
                s3, s2 = tile % 3, tile % 2
                rows = slice(tile * 128, (tile + 1) * 128)
                if tile + 1 < NT:
                    wo_front(tile + 1)
                for half in range(2):
                    pi = (tile * 2 + half) % 4
                    for kc in range(8):
                        S.op("pe", lambda e, kc=kc, pi=pi, s2=s2, half=half: e.matmul(
                            pO[pi][:], lhsT=mT[s2][:, kc, :], rhs=W[:, kc, half * 512:(half + 1) * 512],
                            start=(kc == 0), stop=(kc == 7)), [mTB[s2], WB], [pOB[pi]])
                    hv = hb[s3][:, half * 512:(half + 1) * 512]
                    S.op("dve", lambda e, hv=hv, pi=pi: e.tensor_tensor(out=hv, in0=pO[pi][:], in1=hv, op=ALU.add),
                         [pOB[pi], hbB[s3]], [hbB[s3]])
                if tile == 0:
                    S.op("dve", lambda e, s3=s3: e.memset(hb[s3][0:128 - NMETA, :], 0.0), [hbB[s3]], [hbB[s3]])
                S.dma("sp", dst[rows, :], hb[s3][:], reads=[hbB[s3]])
            S.flush()

    def phase_ffn(layer, final):
        with contextlib.ExitStack() as pes:
            Wu = pes.enter_context(SBT("ff_Wu", [128, 8, 2 * DFF], BF16))
            fq = [0, 3, 8, 15, 22]
            WuBq = [Buf() for _ in range(4)]
            WdBq = [Buf() for _ in range(4)]
            WuB = [WuBq[q] for q in range(4) for _ in range(fq[q], fq[q + 1])]
            WdB = [WdBq[q] for q in range(4) for _ in range(fq[q], fq[q + 1])]
            Wd = pes.enter_context(SBT("ff_Wd", [128, 22, D], BF16))
            wsrc = Wb_up[layer].rearrange("(kc p) c -> p kc c", p=128)
            for q in range(4):
                f0, f1 = fq[q], fq[q + 1]
                S.dma("sp", Wu[:, :, f0 * 128:f1 * 128], wsrc[:, :, f0 * 128:f1 * 128], writes=[WuBq[q]])
                S.dma("sp", Wu[:, :, DFF + f0 * 128:DFF + f1 * 128], wsrc[:, :, DFF + f0 * 128:DFF + f1 * 128],
                      writes=[WuBq[q]])
            for q in range(4):
                f0, f1 = fq[q], fq[q + 1]
                S.dma("sp", Wd[:, f0:f1, :], Wb_dn[layer][f0 * 128:f1 * 128, :].rearrange("(fc p) c -> p fc c", p=128),
                      writes=[WdBq[q]])
            stage = pes.enter_context(SBT("ff_stage", [128, 128], F32))
            stageB = Buf()
            ident, identB = load_const_bf16(pes, "ff_ident", c_ident, [128, 128], stage, stageB)
            cw = pes.enter_context(SBT("ff_cw", [128, 22, 4], F32))
            cwB = Buf()
            S.dma("sp", cw[:], convw[:, layer, :, :], writes=[cwB])
            R = NormT(pes, nyT=1, nhb=2, nyb=4)
            uT = pes.enter_context(SBT("ff_uT", [128, 22, 512], BF16))
            uTB = [Buf() for _ in range(22)]
            carry = pes.enter_context(SBT("ff_carry", [128, 22, 2], F32))
            carryB = [Buf() for _ in range(22)]
            gb = [pes.enter_context(SBT("ff_gb%d" % i, [128, 514], F32)) for i in range(2)]
            gbB = [Buf() for _ in range(2)]
            t1 = [pes.enter_context(SBT("ff_t1%d" % i, [128, 512], F32)) for i in range(2)]
            t1B = [Buf() for _ in range(2)]
            sl = [pes.enter_context(SBT("ff_sl%d" % i, [128, 512], BF16)) for i in range(2)]
            slB = [Buf() for _ in range(2)]
            hr = [pes.enter_context(SBT("ff_hr%d" % i, [128, D], F32)) for i in range(2)]
            hrB = [Buf() for _ in range(2)]
            pG = [pes.enter_context(PST("ff_pG%d" % i, [128, 512], F32)) for i in range(2)]
            pV = [pes.enter_context(PST("ff_pV%d" % i, [128, 512], F32)) for i in range(2)]
            pD = [pes.enter_context(PST("ff_pD%d" % i, [128, 512], F32)) for i in range(2)]
            pGB, pVB, pDB = ([Buf() for _ in range(2)] for _ in range(3))
            if final:
                gf, gfB = load_const_f32(pes, "ff_gf", gfin, [128, D])
                fs = pes.enter_context(SBT("ff_fs", [128, 2], F32))
                fsB = Buf()
                fj = pes.enter_context(SBT("ff_fj", [128, D], BF16))
                fjB = Buf()
            S.op("dve", lambda e: e.memset(carry[:], 0.0), [], carryB)
            k = 0
            tk = 0
            ybs_next = norm_a(R, Hs, groups[0][0], groups[0][1])
            yT, yTB = norm_b(R, ident, identB, ybs_next)
            for gi, (t0, nt) in enumerate(groups):
                N = 128 * nt
                def ff_val(fc):
                    b = (k0 + fc) % 2
                    for kc in range(8):
                        S.op("pe", lambda e, b=b, kc=kc, fc=fc, N=N: e.matmul(
                            pV[b][:, 0:N], lhsT=Wu[:, kc, DFF + fc * 128:DFF + (fc + 1) * 128], rhs=yT[:, kc, 0:N],
                            start=(kc == 0), stop=(kc == 7)), [WuB[fc], yTB], [pVB[b]])

                def ff_stage_a(fc):
                    b = (k0 + fc) % 2
                    for kc in range(8):
                        S.op("pe", lambda e, b=b, kc=kc, fc=fc, N=N: e.matmul(
                            pG[b][:, 0:N], lhsT=Wu[:, kc, fc * 128:(fc + 1) * 128], rhs=yT[:, kc, 0:N],
                            start=(kc == 0), stop=(kc == 7)), [WuB[fc], yTB], [pGB[b]])
                    if fc >= 1:
                        ff_val(fc - 1)
                    g_, gB_ = gb[b], gbB[b]
                    S.op("act", lambda e, g_=g_, b=b, N=N: e.activation(out=g_[:, 2:2 + N], in_=pG[b][:, 0:N],
                                                                        func=AF.Copy), [pGB[b]], [gB_])
                    S.op("dve", lambda e, g_=g_, fc=fc: e.tensor_copy(out=g_[:, 0:2], in_=carry[:, fc, :]),
                         [carryB[fc]], [gB_])
                    S.op("dve", lambda e, g_=g_, fc=fc, N=N: e.tensor_copy(out=carry[:, fc, :], in_=g_[:, N:N + 2]),
                         [gB_], [carryB[fc]])
                    t_, tB_ = t1[b], t1B[b]
                    S.op("act", lambda e, g_=g_, t_=t_, fc=fc, N=N: e.activation(
                        out=t_[:, 0:N], in_=g_[:, 0:N], func=AF.Copy, scale=cw[:, fc, 0:1]),
                        [gB_, cwB], [tB_])

                def ff_stage_b(fc):
                    b = (k0 + fc) % 2
                    g_, gB_ = gb[b], gbB[b]
                    t_, tB_ = t1[b], t1B[b]
                    S.op("dve", lambda e, g_=g_, t_=t_, fc=fc, N=N: e.scalar_tensor_tensor(
                        out=t_[:, 0:N], in0=g_[:, 1:1 + N], scalar=cw[:, fc, 1:2], in1=t_[:, 0:N], op0=ALU.mult,
                        op1=ALU.add), [gB_, cwB, tB_], [tB_])
                    S.op("dve", lambda e, g_=g_, t_=t_, fc=fc, N=N: e.scalar_tensor_tensor(
                        out=t_[:, 0:N], in0=g_[:, 2:2 + N], scalar=cw[:, fc, 2:3], in1=t_[:, 0:N], op0=ALU.mult,
                        op1=ALU.add), [gB_, cwB, tB_], [tB_])
                    s_, sB_ = sl[b], slB[b]
                    S.op("act", lambda e, s_=s_, t_=t_, fc=fc, N=N: e.activation(
                        out=s_[:, 0:N], in_=t_[:, 0:N], func=AF.Silu, bias=cw[:, fc, 3:4], scale=1.0),
                        [tB_, cwB], [sB_])
                    S.op("dve", lambda e, s_=s_, b=b, fc=fc, N=N: e.tensor_tensor(
                        out=uT[:, fc, 0:N], in0=pV[b][:, 0:N], in1=s_[:, 0:N], op=ALU.mult), [pVB[b], sB_],
                        [uTB[fc]])

                k0 = k
                k += 22
                ff_stage_a(0)
                for fc in range(22):
                    if fc + 1 < 22:
                        ff_stage_a(fc + 1)
                    else:
                        ff_val(fc)
                    ff_stage_b(fc)
                have_next = gi + 1 < len(groups)
                ntn = groups[gi + 1][1] if have_next else 0
                t0n = groups[gi + 1][0] if have_next else 0
                nslots = {}
                if have_next:
                    R.kp += 1
                    for j in range(min(2, ntn)):
                        nslots[j] = na_load(R, Hs, t0n + j)
                ybn = {}

                def next_compute(j):
                    ybn[j] = na_compute(R, nslots[j], gi + 1, j)
                    if j + 2 < ntn:
                        nslots[j + 2] = na_load(R, Hs, t0n + j + 2)

                def hr_load(i):
                    tile = t0 + i
                    hs_ = (tk + i) % 2
                    S.dma("sp", hr[hs_][:], Hs[tile * 128:(tile + 1) * 128, :], writes=[hrB[hs_]])

                hr_load(0)
                for i in range(nt):
                    tile = t0 + i
                    rows = slice(tile * 128, (tile + 1) * 128)
                    hs = (tk + i) % 2
                    if i + 1 < nt:
                        hr_load(i + 1)
                    for half in range(2):
                        for fc in range(22):
                            S.op("pe", lambda e, half=half, fc=fc, i=i: e.matmul(
                                pD[half][:], lhsT=uT[:, fc, i * 128:(i + 1) * 128],
                                rhs=Wd[:, fc, half * 512:(half + 1) * 512], start=(fc == 0), stop=(fc == 21)),
                                [uTB[fc], WdB[fc]], [pDB[half]])
                    if i < ntn:
                        next_compute(i)
                    for half in range(2):
                        hv = hr[hs][:, half * 512:(half + 1) * 512]
                        S.op("dve", lambda e, hv=hv, half=half: e.tensor_tensor(out=hv, in0=pD[half][:], in1=hv,
                                                                                op=ALU.add),
                             [pDB[half], hrB[hs]], [hrB[hs]])
                    if not final:
                        if tile == 0:
                            S.op("dve", lambda e, hs=hs: e.memset(hr[hs][0:128 - NMETA, :], 0.0), [hrB[hs]], [hrB[hs]])
                        S.dma("sp", Hs[rows, :], hr[hs][:], reads=[hrB[hs]])
                    elif tile > 0:
                        S.op("act", lambda e, hs=hs: e.activation(out=fj[:], in_=hr[hs][:], func=AF.Square,
                                                                  accum_out=fs[:, 0:1]), [hrB[hs]], [fjB, fsB])
                        S.op("dve", lambda e: e.tensor_scalar(out=fs[:, 0:1], in0=fs[:, 0:1], scalar1=1.0 / D,
                                                              scalar2=EPS, op0=ALU.mult, op1=ALU.add), [fsB], [fsB])
                        S.op("act", lambda e: e.activation(out=fs[:, 0:1], in_=fs[:, 0:1], func=AF.Sqrt), [fsB], [fsB])
                        S.op("dve", lambda e: e.reciprocal(out=fs[:, 0:1], in_=fs[:, 0:1]), [fsB], [fsB])
                        S.op("dve", lambda e, hs=hs: e.scalar_tensor_tensor(
                            out=hr[hs][:], in0=hr[hs][:], scalar=fs[:, 0:1], in1=gf[:], op0=ALU.mult, op1=ALU.mult),
                            [hrB[hs], fsB, gfB], [hrB[hs]])
                        S.dma("sp", out[(tile - 1) * 128:tile * 128, :], hr[hs][:], reads=[hrB[hs]])
                    if i < ntn:
                        yT, yTB = norm_b_tile(R, ident, identB, ybn[i][0], ybn[i][1], i)
                tk += nt
                for j in range(nt, ntn):
                    next_compute(j)
                    yT, yTB = norm_b_tile(R, ident, identB, ybn[j][0], ybn[j][1], j)
            S.flush()

    phase_wprep()
    phase_proj(0, h0)
    phase_retention()
    phase_diff()
    phase_wout(0, h0, Hs)
    phase_ffn(0, final=(depth == 1))
    if depth > 1:
        phase_proj(1, Hs)
        phase_sb()
        phase_wout(1, Hs, Hs)
        phase_ffn(1, final=True)
    es.close()
    return nc, consts


def make_inputs(G, b, x, meta_tokens, mix_norm, ffn_norm, ffn_up, ffn_conv, ffn_conv_b, ffn_down, ab_w_in,
                ab_ret_norm, ab_diff_norm, ab_lam_q1, ab_lam_k1, ab_lam_q2, ab_lam_k2, ab_w_out, c_w_in, c_w_out,
                final_norm, consts, shared):
    P = 128 * (1 + 4 * G)
    f = np.float32
    h0 = np.zeros((P, D), f)
    h0[128 - NMETA:128] = meta_tokens
    h0[128:] = x[b, :P - 128]
    m = {"h0": h0}
    m.update(shared)
    return m


def make_shared(G, meta_tokens, mix_norm, ffn_norm, ffn_up, ffn_conv, ffn_conv_b, ffn_down, ab_w_in,
                ab_ret_norm, ab_diff_norm, ab_lam_q1, ab_lam_k1, ab_lam_q2, ab_lam_k2, ab_w_out, c_w_in, c_w_out,
                final_norm, consts):
    f = np.float32
    A = lambda a: np.ascontiguousarray(np.asarray(a, dtype=f))
    sh = {}
    sh["w_in0"] = A(ab_w_in[0])
    sh["w_out0"] = A(ab_w_out[0])
    sh["w_in1"] = A(c_w_in[0])
    sh["w_out1"] = A(c_w_out[0])
    nl = ffn_up.shape[0]
    for i in range(2):
        j = min(i, nl - 1)
        sh["up%d" % i] = A(ffn_up[j])
        sh["dn%d" % i] = A(ffn_down[j])
    g = np.zeros((128, 32), f)
    for i in range(2):
        j = min(i, nl - 1)
        g[:, i * 8:(i + 1) * 8] = np.asarray(mix_norm[j], f).reshape(8, 128).T
        g[:, 16 + i * 8:16 + (i + 1) * 8] = np.asarray(ffn_norm[j], f).reshape(8, 128).T
    sh["gains"] = g
    cw = np.zeros((128, 2, 22, 4), f)
    for i in range(2):
        j = min(i, nl - 1)
        for t in range(3):
            cw[:, i, :, t] = np.asarray(ffn_conv[j, t], f).reshape(22, 128).T
        cw[:, i, :, 3] = np.asarray(ffn_conv_b[j], f).reshape(22, 128).T
    sh["convw"] = cw
    sh["gret"] = A(np.broadcast_to(np.asarray(ab_ret_norm[0], f)[None, :], (128, 512)))
    sh["gdif"] = A(np.broadcast_to(np.asarray(ab_diff_norm[0], f)[None, :], (128, 512)))
    sh["gfin"] = A(np.broadcast_to(np.asarray(final_norm, f)[None, :], (128, D)))
    lv = np.stack([np.asarray(v[0], f) for v in (ab_lam_q1, ab_lam_k1, ab_lam_q2, ab_lam_k2)], 0)
    sh["lamv"] = A(np.broadcast_to(lv[None], (128, 4, 64)))
    for k in ("ident", "decT", "qdec", "kdec", "tri", "tris", "aq", "kx", "btab", "negU", "negOnes", "padb"):
        sh["c_" + k] = A(consts[k])
    return sh


_CACHE = {}


def run(G, depth, inputs, cores, debug=False):
    key = (G, depth, debug)
    if key not in _CACHE:
        _CACHE[key] = build(G, depth, debug)
    nc, consts = _CACHE[key]
    names = ["meta_tokens", "mix_norm", "ffn_norm", "ffn_up", "ffn_conv", "ffn_conv_b", "ffn_down", "ab_w_in",
             "ab_ret_norm", "ab_diff_norm", "ab_lam_q1", "ab_lam_k1", "ab_lam_q2", "ab_lam_k2", "ab_w_out",
             "c_w_in", "c_w_out", "final_norm"]
    args = [np.asarray(inputs[n]) for n in names]
    shared = make_shared(G, *args, consts)
    x = np.asarray(inputs["x"], np.float32)
    in_maps = [make_inputs(G, b, x, *args, consts, shared) for b in cores]
    res = run_bass_kernel_spmd(nc, in_maps, core_ids=list(range(len(cores))))
    return res


def kernel(**inputs):
    x = np.asarray(inputs["x"])
    B, SEQ, _ = x.shape
    G = SEQ // 512
    res = run(G, 2, inputs, list(range(B)))
    return np.stack([np.asarray(r["out"], np.float32) for r in res.results], 0)
```

```python
import contextlib
import math
import numpy as np
import concourse.bass as bass
import concourse.mybir as mybir
from concourse.bass_utils import run_bass_kernel_spmd

F32 = mybir.dt.float32
BF16 = mybir.dt.bfloat16
AF = mybir.ActivationFunctionType
ALU = mybir.AluOpType
AX = mybir.AxisListType

D = 1024
NMETA = 16
DFF = 2816
EPS = 1e-6
NEG = -30000.0


class Buf:
    __slots__ = ("w", "r", "rd")

    def __init__(self):
        self.w = None
        self.r = {}
        self.rd = []


class Op:
    __slots__ = ("eng", "fn", "deps", "needed", "ms", "dsem", "dkey", "dval", "phase", "isdma")


class Sched:
    CE = ("pe", "act", "dve", "pool")
    ALLE = ("pe", "act", "dve", "pool", "sp")

    def __init__(self, nc, es, nds=20):
        self.nc = nc
        self.sem = {e: es.enter_context(nc.semaphore("s_" + e)) for e in self.CE}
        self.dsems = [es.enter_context(nc.semaphore("dq%d" % i)) for i in range(nds)]
        self.dval = [0] * nds
        self.dlast = [None] * nds
        self.drr = 0
        self.cnt = {e: 0 for e in self.CE}
        self.waited = {e: {} for e in self.ALLE}
        self.ops = {e: [] for e in self.ALLE}
        self.phase = 0
        self.nops = 0

    def _add(self, op, reads, writes):
        deps = set()
        for b in reads:
            if b.w is not None:
                deps.add(b.w)
        for b in writes:
            if b.w is not None:
                deps.add(b.w)
            deps.update(b.r.values())
            deps.update(b.rd)
        for b in reads:
            if op.isdma:
                b.rd.append(op)
            else:
                b.r[op.eng] = op
        for b in writes:
            b.w = op
            b.r = {}
            b.rd = []
        ph = self.phase
        op.deps = [d for d in deps if d is not op and d.phase == ph and
                   not (d.eng == "pe" and op.eng == "pe" and not d.isdma and not op.isdma)]
        self.ops[op.eng].append(op)
        self.nops += 1

    def op(self, eng, fn, reads=(), writes=()):
        o = Op()
        o.eng = eng
        o.fn = fn
        o.needed = False
        o.ms = 0
        o.isdma = False
        o.phase = self.phase
        self._add(o, reads, writes)
        return o

    def dma(self, eng, out, in_, reads=(), writes=()):
        o = Op()
        o.eng = eng
        o.fn = lambda e: e.dma_start(out=out, in_=in_)
        o.needed = False
        o.ms = 0
        o.isdma = True
        o.phase = self.phase
        j = self.drr
        self.drr = (self.drr + 1) % len(self.dsems)
        self.dval[j] += 16
        o.dsem = self.dsems[j]
        o.dkey = "d%d" % j
        o.dval = self.dval[j]
        prev = self.dlast[j]
        self.dlast[j] = o
        self._add(o, reads, writes)
        if prev is not None and prev.phase == self.phase and prev not in o.deps:
            o.deps.append(prev)
        return o

    def flush(self):
        for e in self.ALLE:
            for op in self.ops[e]:
                for d in op.deps:
                    if not d.isdma:
                        d.needed = True
        for e in self.CE:
            for op in self.ops[e]:
                if (not op.isdma) and op.needed:
                    self.cnt[e] += 1
                    op.ms = self.cnt[e]
        sched = self

        def mk(e):
            def body(eng):
                wt = sched.waited[e]
                for op in sched.ops[e]:
                    need = {}
                    for d in op.deps:
                        if d.isdma:
                            k, s, v = d.dkey, d.dsem, d.dval
                        else:
                            k, s, v = d.eng, sched.sem[d.eng], d.ms
                        if k not in need or need[k][1] < v:
                            need[k] = (s, v)
                    for k, (s, v) in need.items():
                        if wt.get(k, 0) < v:
                            eng.wait_ge(s, v)
                            wt[k] = v
                    ins = op.fn(eng)
                    if op.isdma:
                        ins.then_inc(op.dsem, 16)
                    elif op.needed:
                        ins.then_inc(sched.sem[e], 1)
                if e == "sp":
                    for j, s in enumerate(sched.dsems):
                        k = "d%d" % j
                        if wt.get(k, 0) < sched.dval[j]:
                            eng.wait_ge(s, sched.dval[j])
                            wt[k] = sched.dval[j]
            return body

        with self.nc.Block() as block:
            block.tensor(mk("pe"))
            block.scalar(mk("act"))
            block.vector(mk("dve"))
            block.gpsimd(mk("pool"))
            block.sync(mk("sp"))
        self.ops = {e: [] for e in self.ALLE}
        self.phase += 1


def _bf16_round(a):
    a = np.asarray(a, np.float32)
    u = a.view(np.uint32)
    r = ((u >> 16) & 1) + 0x7FFF
    return ((u + r) & 0xFFFF0000).astype(np.uint32).view(np.float32)


def make_consts(P):
    c = {}
    c["ident"] = np.eye(128, dtype=np.float32)
    j = np.arange(128, dtype=np.float64)
    gam = 1.0 - 2.0 ** (-5.0 - np.arange(4, dtype=np.float64))
    lg = np.log(gam)
    dec = np.zeros((128, 4, 128), np.float64)
    for h in range(4):
        rel = j[None, :] - j[:, None]
        dec[:, h, :] = np.where(rel >= 0, np.exp(lg[h] * np.maximum(rel, 0)), 0.0)
    c["decT"] = dec.astype(np.float32)
    qd = np.zeros((128, 4, 512), np.float64)
    for h in range(4):
        qd[:, h, :] = np.exp(lg[h] * ((np.arange(512) % 128) + 1.0))[None, :]
    c["qdec"] = qd.astype(np.float32)
    kd = np.zeros((128, 512), np.float64)
    for h in range(4):
        kd[:, h * 128:(h + 1) * 128] = (np.exp(lg[h] * (127.0 - j)) * 128.0 ** -0.5)[:, None]
    c["kdec"] = kd.astype(np.float32)
    c["cdec"] = [float(np.exp(lg[h] * 128.0)) for h in range(4)]
    c["tri"] = (j[:, None] <= j[None, :]).astype(np.float32)
    c["tris"] = (j[:, None] < j[None, :]).astype(np.float32)
    slopes = 2.0 ** (-8.0 * (np.arange(4) + 1.0) / 4.0)
    scale = 64.0 ** -0.5
    tl = np.arange(512)
    aq = np.zeros((128, 4, 512), np.float32)
    for h in range(4):
        aq[64, h, :] = -(slopes[h] / scale) * (tl - (tl % 2))
        aq[65, h, :] = -(slopes[h] / scale) * (tl % 2)
        aq[66, h, :] = 1.0
        aq[67, h, :] = slopes[h] / scale
    c["aq"] = aq
    kx = np.zeros((128, P), np.float32)
    kx[64, :] = 1.0
    kx[65, :] = 1.0
    kx[66, :128 - NMETA] = NEG
    kx[67, :] = 128.0 * ((np.arange(P) // 128) % 2)
    c["kx"] = kx
    bt = np.zeros((128, 4, 72), np.float32)
    for h in range(4):
        for dl in range(-64, 8):
            bt[:, h, dl + 64] = slopes[h] * (128.0 * dl + j)
    c["btab"] = bt
    sbs = 128.0 ** -0.5
    ninv = -float(_bf16_round(np.float32(1.0 / sbs)))
    c["negU"] = (ninv * (j[:, None] >= j[None, :])).astype(np.float32)
    c["negOnes"] = np.full((128, 128), ninv, np.float32)
    c["sb_fix"] = float(sbs)
    pb = np.zeros((128, 1), np.float32)
    pb[:128 - NMETA] = NEG
    c["padb"] = pb
    return c


def build(G, depth=2, debug=False):
    NT = 1 + 4 * G
    P = 128 * NT
    groups = [(0, 1)] + [(1 + 4 * g, 4) for g in range(G)]
    nc = bass.Bass("TRN2", target_bir_lowering=False)
    es = contextlib.ExitStack()

    def din(name, shape, dt=F32):
        return nc.dram_tensor(name, list(shape), dt, kind="ExternalInput").ap()

    def dscr(name, shape, dt):
        kind = "ExternalOutput" if debug else "Internal"
        return nc.dram_tensor(name, list(shape), dt, kind=kind).ap()

    h0 = din("h0", [P, D])
    w_in0 = din("w_in0", [D, 3584])
    w_out0 = din("w_out0", [D, D])
    w_in1 = din("w_in1", [D, 3072])
    w_out1 = din("w_out1", [D, D])
    ups = [din("up%d" % i, [D, 2 * DFF]) for i in range(2)]
    dns = [din("dn%d" % i, [DFF, D]) for i in range(2)]
    gains = din("gains", [128, 32])
    convw = din("convw", [128, 2, 22, 4])
    gret = din("gret", [128, 512])
    gdif = din("gdif", [128, 512])
    gfin = din("gfin", [128, D])
    lamv = din("lamv", [128, 4, 64])
    c_ident = din("c_ident", [128, 128])
    c_decT = din("c_decT", [128, 4, 128])
    c_qdec = din("c_qdec", [128, 4, 512])
    c_kdec = din("c_kdec", [128, 512])
    c_tri = din("c_tri", [128, 128])
    c_tris = din("c_tris", [128, 128])
    c_aq = din("c_aq", [128, 4, 512])
    c_kx = din("c_kx", [128, P])
    c_btab = din("c_btab", [128, 4, 72])
    c_negU = din("c_negU", [128, 128])
    c_negOnes = din("c_negOnes", [128, 128])
    c_padb = din("c_padb", [128, 1])
    out = nc.dram_tensor("out", [P - 128, D], F32, kind="ExternalOutput").ap()

    Hs = dscr("Hs", [P, D], F32)
    Wb_in0 = dscr("Wb_in0", [D, 3584], BF16)
    Wb_out0 = dscr("Wb_out0", [D, D], BF16)
    Wb_in1 = dscr("Wb_in1", [D, 3072], BF16)
    Wb_out1 = dscr("Wb_out1", [D, D], BF16)
    Wb_up = [dscr("Wb_up%d" % i, [D, 2 * DFF], BF16) for i in range(2)]
    Wb_dn = [dscr("Wb_dn%d" % i, [DFF, D], BF16) for i in range(2)]
    FM = dscr("FM", [16, 128, P], BF16)
    TM = dscr("TM", [P, 2048], BF16)
    MIX = dscr("MIX", [P, D], BF16)
    S = Sched(nc, es)
    consts = make_consts(P)
    cdec = consts["cdec"]
    sb_scale = consts["sb_fix"]

    def rr(k):
        return ("act", "dve")[k % 2]

    def SBT(name, shape, dt):
        return nc.sbuf_tensor("%s_p%d" % (name, S.phase), shape, dt)

    def PST(name, shape, dt):
        return nc.psum_tensor("%s_p%d" % (name, S.phase), shape, dt)

    def evac(eng, out_ap, in_ap, reads, writes, scale=None):
        if eng == "act":
            if scale is None:
                S.op("act", lambda e: e.activation(out=out_ap, in_=in_ap, func=AF.Copy), reads, writes)
            else:
                S.op("act", lambda e: e.activation(out=out_ap, in_=in_ap, func=AF.Copy, scale=float(scale)),
                     reads, writes)
        else:
            if scale is None:
                S.op(eng, lambda e: e.tensor_copy(out=out_ap, in_=in_ap), reads, writes)
            else:
                S.op(eng, lambda e: e.tensor_scalar(out=out_ap, in0=in_ap, scalar1=float(scale), scalar2=None,
                                                    op0=ALU.mult), reads, writes)

    def load_const_bf16(pes, name, src, shape, stage, stageB):
        t = pes.enter_context(SBT(name, shape, BF16))
        b = Buf()
        n = int(np.prod(shape[1:]))
        sv = stage[:, 0:n]
        S.dma("sp", sv, src if len(shape) == 2 else src.rearrange("p a b -> p (a b)"), writes=[stageB])
        tv = t[:] if len(shape) == 2 else t[:].rearrange("p a b -> p (a b)")
        S.op("dve", lambda e: e.tensor_copy(out=tv, in_=sv), [stageB], [b])
        return t, b

    def load_const_f32(pes, name, src, shape):
        t = pes.enter_context(SBT(name, shape, F32))
        b = Buf()
        S.dma("sp", t[:], src, writes=[b])
        return t, b

    def wjob_blocks(jobs):
        for (src, dst, gi, nrc, C, nsp) in jobs:
            cw = C // nsp
            for rc in range(nrc):
                for cs in range(nsp):
                    yield (src[rc * 128:(rc + 1) * 128, cs * cw:(cs + 1) * cw],
                           dst[rc * 128:(rc + 1) * 128, cs * cw:(cs + 1) * cw],
                           None if gi is None else gi * 8 + rc, cw)

    jobs_early = [(w_in0, Wb_in0, 0, 8, 3584, 2)]
    jobs_late = [(w_out0, Wb_out0, None, 8, D, 1), (ups[0], Wb_up[0], 2, 8, 2 * DFF, 2),
                 (dns[0], Wb_dn[0], None, 22, D, 1)]
    if depth > 1:
        jobs_late += [(w_in1, Wb_in1, 1, 8, 3072, 2), (w_out1, Wb_out1, None, 8, D, 1),
                      (ups[1], Wb_up[1], 3, 8, 2 * DFF, 2), (dns[1], Wb_dn[1], None, 22, D, 1)]

    class WConv:
        def __init__(self, pes, nbuf, engs):
            CW = 2816
            self.n = nbuf
            self.stg = [pes.enter_context(SBT("wst%d" % i, [128, CW], F32)) for i in range(nbuf)]
            self.ob = [pes.enter_context(SBT("wob%d" % i, [128, CW], BF16)) for i in range(nbuf)]
            self.stgB = [Buf() for _ in range(nbuf)]
            self.obB = [Buf() for _ in range(nbuf)]
            self.gt, self.gB = load_const_f32(pes, "gains_t", gains, [128, 32])
            self.engs = engs
            self.k = 0

        def load(self, blk):
            i = self.k % self.n
            src, dst, gcol, cw = blk
            S.dma("act", self.stg[i][:, 0:cw], src, writes=[self.stgB[i]])
            self.k += 1
            return (i, blk)

        def convert_store(self, tok):
            i, (src, dst, gcol, cw) = tok
            sv = self.stg[i][:, 0:cw]
            ov = self.ob[i][:, 0:cw]
            eng = self.engs[i % len(self.engs)]
            if gcol is None:
                evac(eng, ov, sv, [self.stgB[i]], [self.obB[i]])
            else:
                gv = self.gt[:, gcol:gcol + 1]
                if eng == "act":
                    S.op("act", lambda e: e.activation(out=ov, in_=sv, func=AF.Copy, scale=gv),
                         [self.stgB[i], self.gB], [self.obB[i]])
                else:
                    S.op("dve", lambda e: e.tensor_scalar(out=ov, in0=sv, scalar1=gv, scalar2=None, op0=ALU.mult),
                         [self.stgB[i], self.gB], [self.obB[i]])
            S.dma("sp", dst, ov, reads=[self.obB[i]])

    def phase_wprep():
        with contextlib.ExitStack() as pes:
            wc = WConv(pes, 4, ("act", "dve"))
            toks = []
            for blk in wjob_blocks(jobs_early):
                toks.append(wc.load(blk))
                if len(toks) > 2:
                    wc.convert_store(toks.pop(0))
            while toks:
                wc.convert_store(toks.pop(0))
            S.flush()

    class NormT:
        def __init__(self, pes, nyT=1, nhb=2, nyb=2):
            self.hb = [pes.enter_context(SBT("n_hb%d" % i, [128, D], F32)) for i in range(nhb)]
            self.hbB = [Buf() for _ in range(nhb)]
            self.yb = [pes.enter_context(SBT("n_yb%d" % i, [128, D], BF16)) for i in range(nyb)]
            self.ybB = [Buf() for _ in range(nyb)]
            self.yT = [pes.enter_context(SBT("n_yT%d" % i, [128, 8, 512], BF16)) for i in range(nyT)]
            self.yTB = [Buf() for _ in range(nyT)]
            self.ss = [pes.enter_context(SBT("n_ss%d" % i, [128, 4], F32)) for i in range(2)]
            self.ssB = [Buf() for _ in range(2)]
            self.pT = [pes.enter_context(PST("n_pT%d" % i, [128, 8, 128], BF16)) for i in range(2)]
            self.pTB = [Buf() for _ in range(2)]
            self.k = 0
            self.ky = 0
            self.g = 0
            self.kp = 0

    def norm_a(R, src, t0, nt):
        gi = R.g
        R.g += 1
        ss = R.ss[gi % 2]
        ssB = R.ssB[gi % 2]
        nhb = len(R.hb)
        res = []
        for i in range(nt):
            s = R.k % nhb
            R.k += 1
            ys = R.ky % len(R.yb)
            R.ky += 1
            hb, hbB, yb, ybB = R.hb[s], R.hbB[s], R.yb[ys], R.ybB[ys]
            tile = t0 + i
            S.dma("act", hb[:], src[tile * 128:(tile + 1) * 128, :], writes=[hbB])
            ssc = ss[:, i:i + 1]
            S.op("act", lambda e, yb=yb, hb=hb, ssc=ssc: e.activation(out=yb[:], in_=hb[:], func=AF.Square,
                                                                      accum_out=ssc), [hbB], [ybB, ssB])
            S.op("dve", lambda e, ssc=ssc: e.tensor_scalar(out=ssc, in0=ssc, scalar1=1.0 / D, scalar2=EPS,
                                                           op0=ALU.mult, op1=ALU.add), [ssB], [ssB])
            S.op("act", lambda e, ssc=ssc: e.activation(out=ssc, in_=ssc, func=AF.Sqrt), [ssB], [ssB])
            S.op("dve", lambda e, ssc=ssc: e.reciprocal(out=ssc, in_=ssc), [ssB], [ssB])
            S.op("act", lambda e, yb=yb, hb=hb, ssc=ssc: e.activation(out=yb[:], in_=hb[:], func=AF.Copy, scale=ssc),
                 [hbB, ssB], [ybB])
            res.append((yb, ybB))
        return res

    def na_load(R, src, tile):
        s_ = R.k % len(R.hb)
        R.k += 1
        S.dma("act", R.hb[s_][:], src[tile * 128:(tile + 1) * 128, :], writes=[R.hbB[s_]])
        return s_

    def na_compute(R, s_, gpar, i):
        ss = R.ss[gpar % 2]
        ssB = R.ssB[gpar % 2]
        ys = R.ky % len(R.yb)
        R.ky += 1
        hb, hbB, yb, ybB = R.hb[s_], R.hbB[s_], R.yb[ys], R.ybB[ys]
        ssc = ss[:, i:i + 1]
        S.op("act", lambda e, yb=yb, hb=hb, ssc=ssc: e.activation(out=yb[:], in_=hb[:], func=AF.Square,
                                                                  accum_out=ssc), [hbB], [ybB, ssB])
        S.op("dve", lambda e, ssc=ssc: e.tensor_scalar(out=ssc, in0=ssc, scalar1=1.0 / D, scalar2=EPS,
                                                       op0=ALU.mult, op1=ALU.add), [ssB], [ssB])
        S.op("act", lambda e, ssc=ssc: e.activation(out=ssc, in_=ssc, func=AF.Sqrt), [ssB], [ssB])
        S.op("dve", lambda e, ssc=ssc: e.reciprocal(out=ssc, in_=ssc), [ssB], [ssB])
        S.op("act", lambda e, yb=yb, hb=hb, ssc=ssc: e.activation(out=yb[:], in_=hb[:], func=AF.Copy, scale=ssc),
             [hbB, ssB], [ybB])
        return (yb, ybB)

    def norm_b(R, ident, identB, ybs):
        gi = R.kp
        R.kp += 1
        yT = R.yT[gi % len(R.yT)]
        yTB = R.yTB[gi % len(R.yT)]
        for i, (yb, ybB) in enumerate(ybs):
            ps = (gi * 4 + i) % 2
            pT, pTB = R.pT[ps], R.pTB[ps]
            for c in range(8):
                S.op("pe", lambda e, pT=pT, yb=yb, c=c: e.transpose(out=pT[:, c, :], in_=yb[:, c * 128:(c + 1) * 128],
                                                                   identity=ident[:]), [ybB, identB], [pTB])
            S.op("dve", lambda e, yT=yT, pT=pT, i=i: e.tensor_copy(out=yT[:, :, i * 128:(i + 1) * 128], in_=pT[:]),
                 [pTB], [yTB])
        return yT, yTB

    def norm_b_tile(R, ident, identB, yb, ybB, i):
        gi = R.kp - 1
        yT = R.yT[gi % len(R.yT)]
        yTB = R.yTB[gi % len(R.yT)]
        ps = (gi * 4 + i) % 2
        pT, pTB = R.pT[ps], R.pTB[ps]
        for c in range(8):
            S.op("pe", lambda e, pT=pT, yb=yb, c=c: e.transpose(out=pT[:, c, :], in_=yb[:, c * 128:(c + 1) * 128],
                                                               identity=ident[:]), [ybB, identB], [pTB])
        S.op("dve", lambda e, yT=yT, pT=pT, i=i: e.tensor_copy(out=yT[:, :, i * 128:(i + 1) * 128], in_=pT[:]),
             [pTB], [yTB])
        return yT, yTB

    def norm_transpose(R, ident, identB, src, t0, nt):
        gi = R.kp
        R.kp += 1
        slots = {0: na_load(R, src, t0)}
        yT = yTB = None
        for i in range(nt):
            if i + 1 < nt:
                slots[i + 1] = na_load(R, src, t0 + i + 1)
            yb, ybB = na_compute(R, slots[i], gi, i)
            yT, yTB = norm_b_tile(R, ident, identB, yb, ybB, i)
        return yT, yTB

    def phase_proj(layer, src):
        if layer == 0:
            Wb, C = Wb_in0, 3584
            kscl = 128.0 ** -0.5
            fm = [(c0, None) for c0 in range(0, 512, 128)] + [(c0, kscl) for c0 in range(512, 1024, 128)] + \
                 [(c0, None) for c0 in range(2048, 3072, 128)]
            tm = [512, 1024, 1536, 3072]
        else:
            Wb, C = Wb_in1, 3072
            fm = [(c0, None) for c0 in range(0, 2048, 128)]
            tm = [2048, 2560]
        nF = len(fm)
        nTM = len(tm)
        with contextlib.ExitStack() as pes:
            W = pes.enter_context(SBT("pj_W", [128, 8, C], BF16))
            WB = Buf()
            for kc in range(8):
                S.dma("sp", W[:, kc, :], Wb[kc * 128:(kc + 1) * 128, :], writes=[WB])
            stage = pes.enter_context(SBT("pj_stage", [128, 128], F32))
            stageB = Buf()
            ident, identB = load_const_bf16(pes, "pj_ident", c_ident, [128, 128], stage, stageB)
            R = NormT(pes, nyT=2, nhb=2, nyb=4)
            FT = [pes.enter_context(SBT("pj_FT%d" % i, [128, nF, 512], BF16)) for i in range(2)]
            FTB = [[Buf() for _ in range(nF)] for _ in range(2)]
            TT = [pes.enter_context(SBT("pj_TT%d" % i, [128, nTM * 512], BF16)) for i in range(2)]
            TTB = [[Buf() for _ in range(nTM)] for _ in range(2)]
            pF = [pes.enter_context(PST("pj_pF%d" % i, [128, 512], F32)) for i in range(3)]
            pFB = [Buf() for _ in range(3)]
            k = 0
            kt = 0
            yT, yTB = norm_transpose(R, ident, identB, src, groups[0][0], groups[0][1])
            for gi, (t0, nt) in enumerate(groups):
                N = 128 * nt
                ft, ftB = FT[gi % 2], FTB[gi % 2]
                for f, (c0, scl) in enumerate(fm):
                    p, pB = pF[k % 3], pFB[k % 3]
                    for kc in range(8):
                        S.op("pe", lambda e, p=p, kc=kc, c0=c0, yT=yT, N=N: e.matmul(
                            p[:, 0:N], lhsT=W[:, kc, c0:c0 + 128], rhs=yT[:, kc, 0:N], start=(kc == 0),
                            stop=(kc == 7)), [WB, yTB], [pB])
                    evac(rr(k), ft[:, f, 0:N], p[:, 0:N], [pB], [ftB[f]], scale=scl)
                    k += 1
                S.dma("sp", FM[0:nF, :, t0 * 128:t0 * 128 + N].rearrange("f p n -> p f n"), ft[:, :, 0:N],
                      reads=ftB)
                have_next = gi + 1 < len(groups)
                ybn = []
                if have_next:
                    t0n, ntn = groups[gi + 1]
                    R.kp += 1
                    nsl = {0: na_load(R, src, t0n)}
                    for j in range(ntn):
                        if j + 1 < ntn:
                            nsl[j + 1] = na_load(R, src, t0n + j + 1)
                        ybn.append(na_compute(R, nsl[j], gi + 1, j))
                for i in range(nt):
                    tt, ttB = TT[kt % 2], TTB[kt % 2]
                    kt += 1
                    for j, c0 in enumerate(tm):
                        p, pB = pF[k % 3], pFB[k % 3]
                        for kc in range(8):
                            S.op("pe", lambda e, p=p, kc=kc, c0=c0, yT=yT, i=i: e.matmul(
                                p[:], lhsT=yT[:, kc, i * 128:(i + 1) * 128], rhs=W[:, kc, c0:c0 + 512],
                                start=(kc == 0), stop=(kc == 7)), [WB, yTB], [pB])
                        evac(rr(k), tt[:, j * 512:(j + 1) * 512], p[:], [pB], [ttB[j]])
                        k += 1
                    tile = t0 + i
                    S.dma("sp", TM[tile * 128:(tile + 1) * 128, 0:nTM * 512], tt[:], reads=ttB)
                if have_next:
                    for j, (yb_, ybB_) in enumerate(ybn):
                        yTn, yTnB = norm_b_tile(R, ident, identB, yb_, ybB_, j)
                    yT, yTB = yTn, yTnB
            S.flush()

    def phase_retention():
        with contextlib.ExitStack() as pes:
            decT, decTB = load_const_f32(pes, "rt_decT", c_decT, [128, 4, 128])
            qdec, qdecB = load_const_f32(pes, "rt_qdec", c_qdec, [128, 4, 512])
            kdec, kdecB = load_const_f32(pes, "rt_kdec", c_kdec, [128, 512])
            gr, grB = load_const_f32(pes, "rt_gret", gret, [128, 512])
            QT = [pes.enter_context(SBT("rt_QT%d" % i, [128, 4, 512], BF16)) for i in range(2)]
            KT = [pes.enter_context(SBT("rt_KT%d" % i, [128, 4, 512], BF16)) for i in range(2)]
            Qd = [pes.enter_context(SBT("rt_Qd%d" % i, [128, 4, 512], BF16)) for i in range(2)]
            TMg = [pes.enter_context(SBT("rt_TM%d" % i, [128, 4, 1536], BF16)) for i in range(2)]
            Kd = [pes.enter_context(SBT("rt_Kd%d" % i, [128, 512], BF16)) for i in range(2)]
            QTB, KTB, QdB, TMB, KdB = ([Buf() for _ in range(2)] for _ in range(5))
            S32 = pes.enter_context(SBT("rt_S32", [128, 4, 128], F32))
            Sb = pes.enter_context(SBT("rt_Sb", [128, 4, 128], BF16))
            S32B, SbB = Buf(), Buf()
            sTm = [pes.enter_context(SBT("rt_sTm%d" % i, [128, 4, 128], BF16)) for i in range(2)]
            sTmB = [Buf() for _ in range(2)]
            sq = pes.enter_context(SBT("rt_sq", [128, 4, 128], F32))
            sqB = Buf()
            st = [pes.enter_context(SBT("rt_st%d" % i, [128, 16], F32)) for i in range(3)]
            stB = [Buf() for _ in range(3)]
            Yr = [pes.enter_context(SBT("rt_Yr%d" % i, [128, 512], F32)) for i in range(2)]
            YrB = [Buf() for _ in range(2)]
            sg = [pes.enter_context(SBT("rt_sg%d" % i, [128, 512], F32)) for i in range(2)]
            sgB = [Buf() for _ in range(2)]
            mo = [pes.enter_context(SBT("rt_mo%d" % i, [128, 512], BF16)) for i in range(2)]
            moB = [Buf() for _ in range(2)]
            psT = [pes.enter_context(PST("rt_psT%d" % i, [128, 4, 128], F32)) for i in range(2)]
            po = [pes.enter_context(PST("rt_po%d" % i, [128, 4, 128], F32)) for i in range(3)]
            pkv = [pes.enter_context(PST("rt_pkv%d" % i, [128, 4, 128], F32)) for i in range(2)]
            psTB, poB, pkvB = ([Buf() for _ in range(3)] for _ in range(3))
            S.op("dve", lambda e: e.memset(S32[:], 0.0), [], [S32B])
            S.op("dve", lambda e: e.memset(Sb[:], 0.0), [], [SbB])
            def gn1_stage(tile, b, i, tb, t3):
                s_ = st[t3]
                sB_ = stB[t3]
                S.op("dve", lambda e, s_=s_, t3=t3: e.tensor_reduce(out=s_[:, 0:4], in_=po[t3][:], axis=AX.X,
                                                                    op=ALU.add), [poB[t3]], [sB_])
                S.op("act", lambda e, t3=t3: e.activation(out=sq[:], in_=po[t3][:], func=AF.Square),
                     [poB[t3]], [sqB])
                S.op("dve", lambda e, s_=s_: e.tensor_reduce(out=s_[:, 4:8], in_=sq[:], axis=AX.X, op=ALU.add),
                     [sqB], [sB_])
                S.op("dve", lambda e, s_=s_: e.tensor_scalar(out=s_[:, 0:4], in0=s_[:, 0:4], scalar1=1.0 / 128,
                                                             scalar2=None, op0=ALU.mult), [sB_], [sB_])
                S.op("dve", lambda e, s_=s_: e.tensor_tensor(out=s_[:, 8:12], in0=s_[:, 0:4], in1=s_[:, 0:4],
                                                             op=ALU.mult), [sB_], [sB_])
                S.op("dve", lambda e, s_=s_: e.scalar_tensor_tensor(
                    out=s_[:, 4:8], in0=s_[:, 4:8], scalar=1.0 / 128, in1=s_[:, 8:12], op0=ALU.mult,
                    op1=ALU.subtract), [sB_], [sB_])
                S.op("dve", lambda e, s_=s_: e.tensor_scalar(out=s_[:, 4:8], in0=s_[:, 4:8], scalar1=EPS,
                                                             scalar2=None, op0=ALU.add), [sB_], [sB_])
                S.op("act", lambda e, s_=s_: e.activation(out=s_[:, 4:8], in_=s_[:, 4:8], func=AF.Sqrt),
                     [sB_], [sB_])

            def gn2_stage(tile, b, i, tb, t3):
                s_ = st[t3]
                sB_ = stB[t3]
                S.op("dve", lambda e, s_=s_: e.reciprocal(out=s_[:, 4:8], in_=s_[:, 4:8]), [sB_], [sB_])
                yr, yrB = Yr[tb], YrB[tb]
                for h in range(4):
                    S.op("dve", lambda e, h=h, yr=yr, s_=s_, tb=tb: e.tensor_scalar(
                        out=yr[:, h * 128:(h + 1) * 128], in0=po[t3][:, h, :], scalar1=s_[:, h:h + 1],
                        scalar2=s_[:, 4 + h:5 + h], op0=ALU.subtract, op1=ALU.mult), [poB[t3], sB_], [yrB])
                S.op("pool", lambda e, yr=yr: e.tensor_tensor(out=yr[:], in0=yr[:], in1=gr[:], op=ALU.mult),
                     [yrB, grB], [yrB])
                S.op("act", lambda e, b=b, i=i, tb=tb: e.activation(out=sg[tb][:], in_=TMg[b][:, i, 1024:1536],
                                                                   func=AF.Silu), [TMB[b]], [sgB[tb]])
                S.op("pool", lambda e, yr=yr, tb=tb: e.tensor_tensor(out=mo[tb][:], in0=yr[:], in1=sg[tb][:],
                                                                     op=ALU.mult), [yrB, sgB[tb]], [moB[tb]])
                S.dma("sp", MIX[tile * 128:(tile + 1) * 128, 0:512], mo[tb][:], reads=[moB[tb]])

            gn_pending = None
            gn_pending2 = None
            tk = 0
            for gi, (t0, nt) in enumerate(groups):
                N = 128 * nt
                b = gi % 2
                S.dma("act", QT[b][:, :, 0:N], FM[0:4, :, t0 * 128:t0 * 128 + N].rearrange("f p n -> p f n"),
                      writes=[QTB[b]])
                S.dma("act", KT[b][:, :, 0:N], FM[4:8, :, t0 * 128:t0 * 128 + N].rearrange("f p n -> p f n"),
                      writes=[KTB[b]])
                S.dma("act", TMg[b][:, 0:nt, :],
                      TM[t0 * 128:t0 * 128 + N, 0:1536].rearrange("(i p) c -> p i c", p=128), writes=[TMB[b]])
                S.op("pool", lambda e, b=b, N=N: e.tensor_tensor(out=Qd[b][:, :, 0:N], in0=QT[b][:, :, 0:N],
                                                                 in1=qdec[:, :, 0:N], op=ALU.mult),
                     [QTB[b], qdecB], [QdB[b]])
                for i in range(nt):
                    tile = t0 + i
                    tb = tk % 2
                    t3 = tk % 3
                    tk += 1
                    cs = slice(i * 128, (i + 1) * 128)
                    for h in range(4):
                        S.op("act", lambda e, b=b, i=i, tb=tb, h=h: e.activation(
                            out=Kd[tb][:, h * 128:(h + 1) * 128], in_=TMg[b][:, i, h * 128:(h + 1) * 128],
                            func=AF.Copy, scale=kdec[:, h * 128:h * 128 + 1]), [TMB[b], kdecB], [KdB[tb]])
                    for h in range(4):
                        S.op("pe", lambda e, b=b, h=h, cs=cs, tb=tb: e.matmul(
                            psT[tb][:, h, :], lhsT=KT[b][:, h, cs], rhs=QT[b][:, h, cs], start=True, stop=True),
                            [KTB[b], QTB[b]], [psTB[tb]])
                    S.op("dve", lambda e, tb=tb: e.tensor_tensor(out=sTm[tb][:], in0=psT[tb][:], in1=decT[:],
                                                                 op=ALU.mult), [psTB[tb], decTB], [sTmB[tb]])
                    for h in range(4):
                        S.op("pe", lambda e, b=b, h=h, i=i, tb=tb, t3=t3: e.matmul(
                            po[t3][:, h, :], lhsT=sTm[tb][:, h, :], rhs=TMg[b][:, i, 512 + h * 128:512 + (h + 1) * 128],
                            start=True, stop=False), [sTmB[tb], TMB[b]], [poB[t3]])
                        S.op("pe", lambda e, b=b, h=h, cs=cs, tb=tb, t3=t3: e.matmul(
                            po[t3][:, h, :], lhsT=Qd[b][:, h, cs], rhs=Sb[:, h, :], start=False, stop=True),
                            [QdB[b], SbB], [poB[t3]])
                    for h in range(4):
                        S.op("pe", lambda e, b=b, h=h, i=i, tb=tb: e.matmul(
                            pkv[tb][:, h, :], lhsT=Kd[tb][:, h * 128:(h + 1) * 128],
                            rhs=TMg[b][:, i, 512 + h * 128:512 + (h + 1) * 128], start=True, stop=True),
                            [KdB[tb], TMB[b]], [pkvB[tb]])
                    for h in range(4):
                        S.op("dve", lambda e, h=h, tb=tb: e.scalar_tensor_tensor(
                            out=S32[:, h, :], in0=S32[:, h, :], scalar=cdec[h], in1=pkv[tb][:, h, :],
                            op0=ALU.mult, op1=ALU.add), [S32B, pkvB[tb]], [S32B])
                    S.op("act", lambda e: e.activation(out=Sb[:], in_=S32[:], func=AF.Copy), [S32B], [SbB])
                    if gn_pending is not None:
                        gn1_stage(*gn_pending)
                    if gn_pending2 is not None:
                        gn2_stage(*gn_pending2)
                    gn_pending2 = gn_pending
                    gn_pending = (tile, b, i, tb, t3)
            if gn_pending is not None:
                gn1_stage(*gn_pending)
            if gn_pending2 is not None:
                gn2_stage(*gn_pending2)
            if gn_pending is not None:
                gn2_stage(*gn_pending)
            S.flush()

    def phase_diff():
        lam_init = 0.8 - 0.6 * math.exp(-0.3 * 0)
        scale = 64.0 ** -0.5
        with contextlib.ExitStack() as pes:
            stage = pes.enter_context(SBT("df_stage", [128, 2048], F32))
            stageB = Buf()
            tri, triB = load_const_bf16(pes, "df_tri", c_tri, [128, 128], stage, stageB)
            aq, aqB = load_const_bf16(pes, "df_aq", c_aq, [128, 4, 512], stage, stageB)
            btab, btabB = load_const_f32(pes, "df_btab", c_btab, [128, 4, 72])
            gd, gdB = load_const_f32(pes, "df_gd", gdif, [128, 512])
            lv, lvB = load_const_f32(pes, "df_lv", lamv, [128, 4, 64])
            lam = pes.enter_context(SBT("df_lam", [128, 8], F32))
            lamB = Buf()
            lj = pes.enter_context(SBT("df_lj", [128, 2, 64], F32))
            ljB = Buf()
            S.op("dve", lambda e: e.tensor_tensor(out=lj[:, 0, :], in0=lv[:, 0, :], in1=lv[:, 1, :], op=ALU.mult),
                 [lvB], [ljB])
            S.op("dve", lambda e: e.tensor_tensor(out=lj[:, 1, :], in0=lv[:, 2, :], in1=lv[:, 3, :], op=ALU.mult),
                 [lvB], [ljB])
            S.op("dve", lambda e: e.tensor_reduce(out=lam[:, 0:2], in_=lj[:], axis=AX.X, op=ALU.add), [ljB], [lamB])
            S.op("act", lambda e: e.activation(out=lam[:, 0:2], in_=lam[:, 0:2], func=AF.Exp), [lamB], [lamB])
            S.op("dve", lambda e: e.tensor_tensor(out=lam[:, 2:3], in0=lam[:, 1:2], in1=lam[:, 0:1],
                                                  op=ALU.subtract), [lamB], [lamB])
            S.op("dve", lambda e: e.tensor_scalar(out=lam[:, 3:4], in0=lam[:, 2:3], scalar1=-lam_init, scalar2=None,
                                                  op0=ALU.add), [lamB], [lamB])
            S.op("dve", lambda e: e.tensor_scalar(out=gd[:], in0=gd[:], scalar1=1.0 - lam_init, scalar2=None,
                                                  op0=ALU.mult), [gdB], [gdB])
            KTm2 = [[pes.enter_context(SBT("df_KT%d_%d" % (hp, m), [128, P], BF16)) for m in range(2)]
                    for hp in range(2)]
            KTm2B = [[Buf() for _ in range(2)] for _ in range(2)]
            Va2 = [pes.enter_context(SBT("df_Va%d" % hp, [128, NT, 132], BF16)) for hp in range(2)]
            Va2B = [Buf() for _ in range(2)]
            QTm = [[pes.enter_context(SBT("df_QT%d_%d" % (m, i), [128, 512], BF16)) for i in range(2)]
                   for m in range(2)]
            QTmB = [[Buf() for _ in range(2)] for _ in range(2)]
            pt = [pes.enter_context(SBT("df_pt%d" % i, [128, 2, 512], BF16)) for i in range(3)]
            ptB = [Buf() for _ in range(3)]
            O1 = pes.enter_context(SBT("df_O1", [128, 4, 128], F32))
            O1B = Buf()
            dif = [pes.enter_context(SBT("df_dif%d" % i, [128, 128], F32)) for i in range(2)]
            difB = [Buf() for _ in range(2)]
            dsq = pes.enter_context(SBT("df_dsq", [128, 128], F32))
            dsqB = Buf()
            rd = [pes.enter_context(SBT("df_rd%d" % i, [128, 4], F32)) for i in range(2)]
            rdB = [Buf() for _ in range(2)]
            mo = [pes.enter_context(SBT("df_mo%d" % i, [128, 128], BF16)) for i in range(2)]
            moB = [Buf() for _ in range(2)]
            psT = [pes.enter_context(PST("df_psT%d" % i, [128, 2, 512], F32)) for i in range(2)]
            psTB = [Buf() for _ in range(2)]
            acc = [[pes.enter_context(PST("df_acc%d_%d" % (a, j), [128, 2, 256], F32)) for j in range(2)]
                   for a in range(2)]
            accB = [[Buf() for _ in range(2)] for _ in range(2)]
            for hp in range(2):
                S.op("pool", lambda e, hp=hp: e.memset(Va2[hp][:], 1.0), [], [Va2B[hp]])
            for m in range(2):
                for i in range(2):
                    S.op("pool", lambda e, m=m, i=i: e.memset(QTm[m][i][:], 0.0), [], [QTmB[m][i]])
            for cc in range(0, P, 2048):
                ce = min(P, cc + 2048)
                S.dma("sp", stage[64:68, 0:ce - cc], c_kx[64:68, cc:ce], writes=[stageB])
                for hp in range(2):
                    for m in range(2):
                        S.op("pool", lambda e, m=m, hp=hp, cc=cc, ce=ce: e.tensor_copy(
                            out=KTm2[hp][m][64:68, cc:ce], in_=stage[64:68, 0:ce - cc]), [stageB], [KTm2B[hp][m]])
            wc = WConv(pes, 2, ("dve",))
            wblocks = wjob_blocks(jobs_late)
            wtoks = []

            def wstep():
                blk = next(wblocks, None)
                if blk is not None:
                    wtoks.append(wc.load(blk))
                if wtoks and (len(wtoks) > 1 or blk is None):
                    wc.convert_store(wtoks.pop(0))
                return blk is not None or bool(wtoks)
            segs = [(h, gi) for h in range(4) for gi in range(len(groups))]

            def load_kv(h):
                hp = h % 2
                for m in range(2):
                    S.dma("act", KTm2[hp][m][0:64, :], FM[12 + h, m * 64:(m + 1) * 64, :], writes=[KTm2B[hp][m]])
                    S.dma("act", KTm2[hp][m][68:128, :], FM[12 + h, (1 - m) * 64:(1 - m) * 64 + 60, :],
                          writes=[KTm2B[hp][m]])
                S.dma("act", Va2[hp][:, :, 0:128],
                      TM[:, 1536 + h * 128:1536 + (h + 1) * 128].rearrange("(i p) c -> p i c", p=128),
                      writes=[Va2B[hp]])

            def load_q(k):
                h, gi = segs[k]
                t0, nt = groups[gi]
                N = 128 * nt
                qb = k % 2
                for m in range(2):
                    S.dma("act", QTm[m][qb][0:64, 0:N], FM[8 + h, m * 64:(m + 1) * 64, t0 * 128:t0 * 128 + N],
                          writes=[QTmB[m][qb]])
                    if k < 2 or segs[k - 2][0] != h:
                        S.op("pool", lambda e, m=m, qb=qb, h=h: e.tensor_copy(out=QTm[m][qb][64:68, :],
                                                                              in_=aq[64:68, h, :]),
                             [aqB], [QTmB[m][qb]])

            load_kv(0)
            load_q(0)
            d4 = [pes.enter_context(SBT("df_d4_%d" % i, [128, 4, 128], F32)) for i in range(2)]
            d4B = [Buf() for _ in range(2)]
            dsq4 = pes.enter_context(SBT("df_dsq4", [128, 4, 128], F32))
            dsq4B = Buf()
            rr4 = [pes.enter_context(SBT("df_rr4_%d" % i, [128, 16], F32)) for i in range(2)]
            rr4B = [Buf() for _ in range(2)]
            mo4 = [pes.enter_context(SBT("df_mo4_%d" % i, [128, 4, 128], BF16)) for i in range(2)]
            mo4B = [Buf() for _ in range(2)]
            epsb = pes.enter_context(SBT("df_epsb", [128, 1], F32))
            epsbB = Buf()
            S.op("dve", lambda e: e.memset(epsb[:], EPS), [], [epsbB])
            fcount = [0]
            pending = None

            def make_final(h, t0, nt, m, a):
                def emit():
                    fp = fcount[0] % 2
                    fcount[0] += 1
                    rr, rrB = rr4[fp], rr4B[fp]
                    accs = [(acc[a][i // 2], accB[a][i // 2]) for i in range(nt)]
                    for i in range(nt):
                        ac, acB = accs[i]
                        S.op("dve", lambda e, ac=ac, i=i: e.tensor_scalar(
                            out=rr[:, i:i + 1], in0=ac[:, i % 2, 128:129], scalar1=1e-30, scalar2=None, op0=ALU.add),
                            [acB], [rrB])
                    S.op("dve", lambda e: e.reciprocal(out=rr[:, 0:nt], in_=rr[:, 0:nt]), [rrB], [rrB])
                    if m == 0:
                        for i in range(nt):
                            ac, acB = accs[i]
                            S.op("dve", lambda e, ac=ac, i=i: e.tensor_scalar(
                                out=O1[:, i, :], in0=ac[:, i % 2, 0:128], scalar1=rr[:, i:i + 1], scalar2=None,
                                op0=ALU.mult), [acB, rrB], [O1B])
                        return
                    d_, dB_ = d4[fp], d4B[fp]
                    for i in range(nt):
                        ac, acB = accs[i]
                        S.op("dve", lambda e, ac=ac, i=i: e.tensor_scalar(
                            out=d_[:, i, :], in0=ac[:, i % 2, 0:128], scalar1=rr[:, i:i + 1], scalar2=lam[:, 3:4],
                            op0=ALU.mult, op1=ALU.mult), [acB, rrB, lamB], [dB_])
                    S.op("dve", lambda e: e.tensor_tensor(out=d_[:, 0:nt, :], in0=d_[:, 0:nt, :], in1=O1[:, 0:nt, :],
                                                          op=ALU.add), [dB_, O1B], [dB_])
                    S.op("act", lambda e: e.activation(out=dsq4[:, 0:nt, :], in_=d_[:, 0:nt, :], func=AF.Square),
                         [dB_], [dsq4B])
                    S.op("dve", lambda e: e.tensor_reduce(out=rr[:, 4:4 + nt], in_=dsq4[:, 0:nt, :], axis=AX.X,
                                                          op=ALU.add), [dsq4B], [rrB])
                    S.op("act", lambda e: e.activation(out=rr[:, 8:8 + nt], in_=rr[:, 4:4 + nt], func=AF.Ln,
                                                       bias=epsb[:, 0:1], scale=1.0 / 128), [rrB, epsbB], [rrB])
                    S.op("act", lambda e: e.activation(out=rr[:, 8:8 + nt], in_=rr[:, 8:8 + nt], func=AF.Exp,
                                                       scale=-0.5), [rrB], [rrB])
                    mo_, moB_ = mo4[fp], mo4B[fp]
                    for i in range(nt):
                        S.op("dve", lambda e, i=i: e.scalar_tensor_tensor(
                            out=mo_[:, i, :], in0=d_[:, i, :], scalar=rr[:, 8 + i:9 + i],
                            in1=gd[:, h * 128:(h + 1) * 128], op0=ALU.mult, op1=ALU.mult), [dB_, rrB, gdB], [moB_])
                    S.dma("sp", MIX[t0 * 128:(t0 + nt) * 128, 512 + h * 128:512 + (h + 1) * 128].rearrange(
                        "(i p) c -> p i c", p=128), mo_[:, 0:nt, :], reads=[moB_])
                return emit

            gpend = []

            def dflush(keep):
                while len(gpend) > keep:
                    kbs2, rel2, c02, pp2, pp2B, a2, t02, nt2, Va_, VaB_, fin = gpend.pop(0)
                    for j, kb2 in enumerate(kbs2):
                        for i in range(max(rel2, 0), nt2):
                            ac, acB = acc[a2][i // 2], accB[a2][i // 2]
                            S.op("pe", lambda e, ac=ac, i=i, j=j, pp2=pp2, c02=c02, kb2=kb2, t02=t02, Va_=Va_:
                                 e.matmul(ac[:, i % 2, 0:129], lhsT=pp2[:, j, i * 128 - c02:(i + 1) * 128 - c02],
                                          rhs=Va_[:, kb2, 0:129], start=(kb2 == 0 and i % 2 == 0),
                                          stop=(kb2 == t02 + i), skip_group_check=True),
                                 [pp2B, VaB_], [acB])
                    if fin is not None:
                        finq.append([fin, 2])
                for ent in list(finq):
                    ent[1] -= 1
                    if ent[1] <= 0 or keep == 0:
                        ent[0]()
                        finq.remove(ent)

            finq = []

            fk = 0
            ak = 0
            for h in range(4):
                KTm, KTmB, Va, VaB = KTm2[h % 2], KTm2B[h % 2], Va2[h % 2], Va2B[h % 2]
                for gi, (t0, nt) in enumerate(groups):
                    N = 128 * nt
                    ksg = h * len(groups) + gi
                    qb = ksg % 2
                    if gi == 1 and h + 1 < 4:
                        load_kv(h + 1)
                    if ksg + 1 < len(segs):
                        load_q(ksg + 1)
                    for m in range(2):
                        a = ak % 2
                        ak += 1
                        nkb = t0 + nt
                        dunits = []
                        kb = 0
                        while kb < nkb:
                            if kb % 2 == 0 and kb + 1 < t0:
                                dunits.append([kb, kb + 1])
                                kb += 2
                            else:
                                dunits.append([kb])
                                kb += 1
                        nun = len(dunits)
                        for ui in range(nun):
                            if ui == min(2, nun - 1):
                                wstep()
                            kbs = dunits[ui]
                            nb = len(kbs)
                            rel = kbs[0] - t0
                            c0 = 128 * max(rel, 0)
                            ncol = N - c0
                            ps, psB = psT[fk % 2], psTB[fk % 2]
                            pp, ppB = pt[fk % 3], ptB[fk % 3]
                            fk += 1
                            for j, kb in enumerate(kbs):
                                S.op("pe", lambda e, ps=ps, m=m, kb=kb, j=j, qb=qb, c0=c0, N=N, ncol=ncol, KTm=KTm:
                                     e.matmul(ps[:, j, 0:ncol], lhsT=KTm[m][0:128, kb * 128:(kb + 1) * 128],
                                              rhs=QTm[m][qb][0:128, c0:N], start=True, stop=True),
                                     [KTmB[m], QTmB[m][qb]], [psB])
                            be = kbs[0] - (kbs[0] % 2) - t0
                            bv = btab[:, h, be + 64:be + 65]
                            S.op("act", lambda e, pp=pp, ps=ps, ncol=ncol, bv=bv, nb=nb: e.activation(
                                out=pp[:, 0:nb, 0:ncol], in_=ps[:, 0:nb, 0:ncol], func=AF.Exp, bias=bv,
                                scale=scale), [psB, btabB], [ppB])
                            if rel >= 0:
                                S.op("pool", lambda e, pp=pp: e.tensor_tensor(out=pp[:, 0, 0:128],
                                                                              in0=pp[:, 0, 0:128],
                                                                              in1=tri[:], op=ALU.mult),
                                     [ppB, triB], [ppB])
                            last = (ui == nun - 1)
                            gpend.append((kbs, rel, c0, pp, ppB, a, t0, nt, Va, VaB,
                                          make_final(h, t0, nt, m, a) if last else None))
                            dflush(2)
            dflush(0)
            while wstep():
                pass
            S.flush()

    def phase_sb():
        with contextlib.ExitStack() as pes:
            stage = pes.enter_context(SBT("sb_stage", [128, 128], F32))
            stageB = Buf()
            tris, trisB = load_const_bf16(pes, "sb_tris", c_tris, [128, 128], stage, stageB)
            negU, negUB = load_const_bf16(pes, "sb_negU", c_negU, [128, 128], stage, stageB)
            negO, negOB = load_const_bf16(pes, "sb_negO", c_negOnes, [128, 128], stage, stageB)
            padb, padbB = load_const_f32(pes, "sb_padb", c_padb, [128, 1])
            zb = pes.enter_context(SBT("sb_zb", [128, 1], F32))
            ob1 = pes.enter_context(SBT("sb_ob1", [128, 1], F32))
            zbB = Buf()
            S.op("dve", lambda e: e.memset(zb[:], 0.0), [], [zbB])
            S.op("dve", lambda e: e.memset(ob1[:], 1.0), [], [zbB])
            KT2 = [pes.enter_context(SBT("sb_KT%d" % i, [128, P], BF16)) for i in range(2)]
            KT2B = [Buf() for _ in range(2)]
            V2 = [pes.enter_context(SBT("sb_V%d" % i, [128, NT, 128], BF16)) for i in range(2)]
            V2B = [Buf() for _ in range(2)]
            QT = [pes.enter_context(SBT("sb_QT%d" % i, [128, 512], BF16)) for i in range(3)]
            QTB = [Buf() for _ in range(3)]
            A32 = [pes.enter_context(SBT("sb_A32_%d" % i, [128, 512], F32)) for i in range(2)]
            A16 = [[pes.enter_context(SBT("sb_A16_%d_%d" % (i, j), [128, 512], BF16)) for j in range(2)]
                   for i in range(2)]
            A32B = [Buf() for _ in range(2)]
            A16B = [[Buf() for _ in range(2)] for _ in range(2)]
            apar = [0, 0]
            NB = 3
            u = [pes.enter_context(SBT("sb_u%d" % i, [128, 2, 512], F32)) for i in range(2)]
            uB = [Buf() for _ in range(2)]
            sp = [pes.enter_context(SBT("sb_sp%d" % i, [128, 2, 512], BF16)) for i in range(NB)]
            spB = [Buf() for _ in range(NB)]
            w = [pes.enter_context(SBT("sb_w%d" % i, [128, 2, 512], BF16)) for i in range(NB)]
            wB = [Buf() for _ in range(NB)]
            mo = [pes.enter_context(SBT("sb_mo%d" % i, [128, 4, 128], BF16)) for i in range(2)]
            moB = [Buf() for _ in range(2)]
            pz = [pes.enter_context(PST("sb_pz%d" % i, [128, 2, 512], F32)) for i in range(1)]
            pzB = [Buf() for _ in range(1)]
            pE = [pes.enter_context(PST("sb_pE%d" % i, [128, 2, 512], F32)) for i in range(2)]
            pEB = [Buf() for _ in range(2)]
            acc = [pes.enter_context(PST("sb_acc%d" % i, [128, 4, 128], F32)) for i in range(2)]
            accB = [Buf() for _ in range(2)]
            units = []
            for h in range(8):
                for gi, (t0, nt) in enumerate(groups):
                    for kb in range(t0 + nt - 1, t0 - 1, -1):
                        units.append((h, gi, t0, nt, [kb]))
                    kb = t0 - 1
                    while kb >= 0:
                        if kb >= 2:
                            units.append((h, gi, t0, nt, [kb, kb - 1]))
                            kb -= 2
                        else:
                            units.append((h, gi, t0, nt, [kb]))
                            kb -= 1
            n = len(units)
            st = {}
            gcount = -1
            cur_h = -1
            cur_g = None
            ssegs = [(h, gi) for h in range(8) for gi in range(len(groups))]

            def sb_load_kv(h):
                S.dma("act", KT2[h % 2][:, :], FM[8 + h, :, :], writes=[KT2B[h % 2]])
                S.dma("act", V2[h % 2][:, :, :], TM[:, h * 128:(h + 1) * 128].rearrange("(i p) c -> p i c", p=128),
                      writes=[V2B[h % 2]])

            def sb_load_q(k):
                h, gi = ssegs[k]
                t0, nt = groups[gi]
                S.dma("act", QT[k % 3][:, 0:128 * nt], FM[h, :, t0 * 128:(t0 + nt) * 128], writes=[QTB[k % 3]])

            sb_load_kv(0)
            sb_load_q(0)

            def stage1(idx):
                nonlocal gcount, cur_h, cur_g
                h, gi, t0, nt, kbs = units[idx]
                N = 128 * nt
                nb = len(kbs)
                if (h, gi) != cur_g:
                    cur_g = (h, gi)
                    gcount += 1
                    g2 = gcount % 2
                    if gi == 1 and h + 1 < 8:
                        sb_load_kv(h + 1)
                    if gcount + 1 < len(ssegs):
                        sb_load_q(gcount + 1)
                    S.op("pool", lambda e, g2=g2: e.memset(A32[g2][:], 0.0), [], [A32B[g2]])
                    S.op("pool", lambda e, g2=g2: e.memset(A16[g2][0][:], 0.0), [], [A16B[g2][0]])
                    S.op("pool", lambda e, g2=g2: e.memset(A16[g2][1][:], 0.0), [], [A16B[g2][1]])
                g2 = gcount % 2
                q3 = gcount % 3
                rel = kbs[0] - t0
                c0 = 128 * max(rel, 0)
                ncol = N - c0
                first = (kbs[0] == t0 + nt - 1)
                z, zB = pz[0], pzB[0]
                uu, uuB = u[idx % 2], uB[idx % 2]
                s_, sB_ = sp[idx % NB], spB[idx % NB]
                bias = padb[:, 0:1] if kbs[-1] == 0 and nb == 1 else zb[:, 0:1]
                assert not (0 in kbs and nb == 2)
                for j, kb in enumerate(kbs):
                    S.op("pe", lambda e, j=j, kb=kb: e.matmul(z[:, j, 0:ncol], lhsT=KT2[h % 2][:, kb * 128:(kb + 1) * 128],
                                                              rhs=QT[q3][:, c0:N], start=True, stop=True),
                         [KT2B[h % 2], QTB[q3]], [zB])
                S.op("act", lambda e: e.activation(out=uu[:, 0:nb, 0:ncol], in_=z[:, 0:nb, 0:ncol], func=AF.Exp,
                                                   bias=bias, scale=sb_scale), [zB, padbB, zbB], [uuB])
                S.op("act", lambda e: e.activation(out=s_[:, 0:nb, 0:ncol], in_=uu[:, 0:nb, 0:ncol], func=AF.Ln,
                                                   bias=ob1[:, 0:1], scale=1.0), [uuB, zbB], [sB_])
                if rel >= 0:
                    S.op("pool", lambda e: e.tensor_tensor(out=s_[:, 0, 0:128], in0=s_[:, 0, 0:128], in1=tris[:],
                                                           op=ALU.mult), [sB_, trisB], [sB_])
                st[idx] = dict(q3=q3, g2=g2, rel=rel, c0=c0, ncol=ncol, first=first, N=N, bias=bias)

            def stage2(idx):
                h, gi, t0, nt, kbs = units[idx]
                nb = len(kbs)
                d = st[idx]
                q3 = d["q3"]
                g2, c0, ncol, N, first, rel, bias = d["g2"], d["c0"], d["ncol"], d["N"], d["first"], d["rel"], d["bias"]
                E, EB = pE[idx % 2], pEB[idx % 2]
                s_, sB_ = sp[idx % NB], spB[idx % NB]
                w_, wB_ = w[idx % NB], wB[idx % NB]
                for j, kb in enumerate(kbs):
                    S.op("pe", lambda e, j=j, kb=kb: e.matmul(E[:, j, 0:ncol], lhsT=KT2[h % 2][:, kb * 128:(kb + 1) * 128],
                                                              rhs=QT[q3][:, c0:N], start=True, stop=False),
                         [KT2B[h % 2], QTB[q3]], [EB])
                    S.op("pe", lambda e, j=j: e.matmul(E[:, j, 0:ncol], lhsT=negU[:], rhs=s_[:, j, 0:ncol], start=False,
                                                       stop=first), [negUB, sB_], [EB])
                    ap_ = apar[g2]
                    if not first:
                        S.op("pe", lambda e, j=j, ap_=ap_: e.matmul(E[:, j, 0:ncol], lhsT=negO[:],
                                                                    rhs=A16[g2][ap_][:, c0:N], start=False, stop=True),
                             [negOB, A16B[g2][ap_]], [EB])
                    if kb > 0:
                        S.op("dve", lambda e, j=j: e.tensor_tensor(out=A32[g2][:, c0:N], in0=A32[g2][:, c0:N],
                                                                   in1=s_[:, j, 0:ncol], op=ALU.add),
                             [A32B[g2], sB_], [A32B[g2]])
                        S.op("dve", lambda e, ap_=ap_: e.tensor_copy(out=A16[g2][1 - ap_][:, c0:N],
                                                                     in_=A32[g2][:, c0:N]),
                             [A32B[g2]], [A16B[g2][1 - ap_]])
                        apar[g2] = 1 - ap_
                S.op("act", lambda e: e.activation(out=w_[:, 0:nb, 0:ncol], in_=E[:, 0:nb, 0:ncol], func=AF.Exp,
                                                   bias=bias, scale=sb_scale), [EB, padbB, zbB], [wB_])
                if rel >= 0:
                    S.op("pool", lambda e: e.tensor_tensor(out=w_[:, 0, 0:128], in0=w_[:, 0, 0:128], in1=tris[:],
                                                           op=ALU.mult), [wB_, trisB], [wB_])

            def stage3(idx):
                h, gi, t0, nt, kbs = units[idx]
                d = st.pop(idx)
                g2, c0, rel = d["g2"], d["c0"], d["rel"]
                w_, wB_ = w[idx % NB], wB[idx % NB]
                ac, acB = acc[g2], accB[g2]
                for j, kb in enumerate(kbs):
                    for i in range(max(rel, 0), nt):
                        S.op("pe", lambda e, i=i, j=j, kb=kb: e.matmul(
                            ac[:, i, :], lhsT=w_[:, j, i * 128 - c0:(i + 1) * 128 - c0], rhs=V2[h % 2][:, kb, :],
                            start=(kb == t0 + nt - 1 and i == nt - 1), stop=(kb == 0), skip_group_check=True),
                            [wB_, V2B[h % 2]], [acB])
                if kbs[-1] == 0:
                    mo_, moB_ = mo[g2], moB[g2]
                    evac("dve", mo_[:, 0:nt, :], ac[:, 0:nt, :], [acB], [moB_])
                    S.dma("sp", MIX[t0 * 128:(t0 + nt) * 128, h * 128:(h + 1) * 128].rearrange(
                        "(i p) c -> p i c", p=128), mo_[:, 0:nt, :], reads=[moB_])

            for s in range(n + 2):
                if s < n:
                    stage1(s)
                if 0 <= s - 1 < n:
                    stage2(s - 1)
                if 0 <= s - 2 < n:
                    stage3(s - 2)
            S.flush()

    def phase_wout(layer, src, dst):
        Wb = Wb_out0 if layer == 0 else Wb_out1
        with contextlib.ExitStack() as pes:
            W = pes.enter_context(SBT("wo_W", [128, 8, D], BF16))
            WB = Buf()
            S.dma("sp", W[:], Wb.rearrange("(kc p) c -> p kc c", p=128), writes=[WB])
            stage = pes.enter_context(SBT("wo_stage", [128, 128], F32))
            stageB = Buf()
            ident, identB = load_const_bf16(pes, "wo_ident", c_ident, [128, 128], stage, stageB)
            mx = [pes.enter_context(SBT("wo_mx%d" % i, [128, D], BF16)) for i in range(3)]
            mxB = [Buf() for _ in range(3)]
            hb = [pes.enter_context(SBT("wo_hb%d" % i, [128, D], F32)) for i in range(3)]
            hbB = [Buf() for _ in range(3)]
            mT = [pes.enter_context(SBT("wo_mT%d" % i, [128, 8, 128], BF16)) for i in range(2)]
            mTB = [Buf() for _ in range(2)]
            pT = [pes.enter_context(PST("wo_pT%d" % i, [128, 8, 128], BF16)) for i in range(2)]
            pTB = [Buf() for _ in range(2)]
            pO = [pes.enter_context(PST("wo_pO%d" % i, [128, 512], F32)) for i in range(4)]
            pOB = [Buf() for _ in range(4)]
            for tile in range(NT):
                s3, s2 = tile % 3, tile % 2
                rows = slice(tile * 128, (tile + 1) * 128)
                S.dma("act", mx[s3][:], MIX[rows, :], writes=[mxB[s3]])
                S.dma("act", hb[s3][:], src[rows, :], writes=[hbB[s3]])
                for c in range(8):
                    S.op("pe", lambda e, c=c, s2=s2, s3=s3: e.transpose(out=pT[s2][:, c, :],
                                                                       in_=mx[s3][:, c * 128:(c + 1) * 128],
                                                                       identity=ident[:]), [mxB[s3], identB], [pTB[s2]])
                evac(rr(tile), mT[s2][:], pT[s2][:], [pTB[s2]], [mTB[s2]])
                for half in range(2):
                    pi = (tile * 2 + half) % 4
                    for kc in range(8):
                        S.op("pe", lambda e, kc=kc, pi=pi, s2=s2, half=half: e.matmul(
                            pO[pi][:], lhsT=mT[s2][:, kc, :], rhs=W[:, kc, half * 512:(half + 1) * 512],
                            start=(kc == 0), stop=(kc == 7)), [mTB[s2], WB], [pOB[pi]])
                    hv = hb[s3][:, half * 512:(half + 1) * 512]
                    S.op("dve", lambda e, hv=hv, pi=pi: e.tensor_tensor(out=hv, in0=pO[pi][:], in1=hv, op=ALU.add),
                         [pOB[pi], hbB[s3]], [hbB[s3]])
                if tile == 0:
                    S.op("dve", lambda e, s3=s3: e.memset(hb[s3][0:128 - NMETA, :], 0.0), [hbB[s3]], [hbB[s3]])
                S.dma("sp", dst[rows, :], hb[s3][:], reads=[hbB[s3]])
            S.flush()

    def phase_ffn(layer, final):
        with contextlib.ExitStack() as pes:
            Wu = pes.enter_context(SBT("ff_Wu", [128, 8, 2 * DFF], BF16))
            fq = [0, 3, 8, 15, 22]
            WuBq = [Buf() for _ in range(4)]
            WdBq = [Buf() for _ in range(4)]
            WuB = [WuBq[q] for q in range(4) for _ in range(fq[q], fq[q + 1])]
            WdB = [WdBq[q] for q in range(4) for _ in range(fq[q], fq[q + 1])]
            Wd = pes.enter_context(SBT("ff_Wd", [128, 22, D], BF16))
            wsrc = Wb_up[layer].rearrange("(kc p) c -> p kc c", p=128)
            for q in range(4):
                f0, f1 = fq[q], fq[q + 1]
                S.dma("sp", Wu[:, :, f0 * 128:f1 * 128], wsrc[:, :, f0 * 128:f1 * 128], writes=[WuBq[q]])
                S.dma("sp", Wu[:, :, DFF + f0 * 128:DFF + f1 * 128], wsrc[:, :, DFF + f0 * 128:DFF + f1 * 128],
                      writes=[WuBq[q]])
            for q in range(4):
                f0, f1 = fq[q], fq[q + 1]
                S.dma("sp", Wd[:, f0:f1, :], Wb_dn[layer][f0 * 128:f1 * 128, :].rearrange("(fc p) c -> p fc c", p=128),
                      writes=[WdBq[q]])
            stage = pes.enter_context(SBT("ff_stage", [128, 128], F32))
            stageB = Buf()
            ident, identB = load_const_bf16(pes, "ff_ident", c_ident, [128, 128], stage, stageB)
            cw = pes.enter_context(SBT("ff_cw", [128, 22, 4], F32))
            cwB = Buf()
            S.dma("sp", cw[:], convw[:, layer, :, :], writes=[cwB])
            R = NormT(pes, nyT=1, nhb=2, nyb=4)
            uT = pes.enter_context(SBT("ff_uT", [128, 22, 512], BF16))
            uTB = [Buf() for _ in range(22)]
            carry = pes.enter_context(SBT("ff_carry", [128, 22, 2], F32))
            carryB = [Buf() for _ in range(22)]
            gb = [pes.enter_context(SBT("ff_gb%d" % i, [128, 514], F32)) for i in range(2)]
            gbB = [Buf() for _ in range(2)]
            t1 = [pes.enter_context(SBT("ff_t1%d" % i, [128, 512], F32)) for i in range(2)]
            t1B = [Buf() for _ in range(2)]
            sl = [pes.enter_context(SBT("ff_sl%d" % i, [128, 512], BF16)) for i in range(2)]
            slB = [Buf() for _ in range(2)]
            hr = [pes.enter_context(SBT("ff_hr%d" % i, [128, D], F32)) for i in range(2)]
            hrB = [Buf() for _ in range(2)]
            pG = [pes.enter_context(PST("ff_pG%d" % i, [128, 512], F32)) for i in range(2)]
            pV = [pes.enter_context(PST("ff_pV%d" % i, [128, 512], F32)) for i in range(2)]
            pD = [pes.enter_context(PST("ff_pD%d" % i, [128, 512], F32)) for i in range(2)]
            pGB, pVB, pDB = ([Buf() for _ in range(2)] for _ in range(3))
            if final:
                gf, gfB = load_const_f32(pes, "ff_gf", gfin, [128, D])
                fs = pes.enter_context(SBT("ff_fs", [128, 2], F32))
                fsB = Buf()
                fj = pes.enter_context(SBT("ff_fj", [128, D], BF16))
                fjB = Buf()
            S.op("dve", lambda e: e.memset(carry[:], 0.0), [], carryB)
            k = 0
            tk = 0
            ybs_next = norm_a(R, Hs, groups[0][0], groups[0][1])
            yT, yTB = norm_b(R, ident, identB, ybs_next)
            for gi, (t0, nt) in enumerate(groups):
                N = 128 * nt
                def ff_val(fc):
                    b = (k0 + fc) % 2
                    for kc in range(8):
                        S.op("pe", lambda e, b=b, kc=kc, fc=fc, N=N: e.matmul(
                            pV[b][:, 0:N], lhsT=Wu[:, kc, DFF + fc * 128:DFF + (fc + 1) * 128], rhs=yT[:, kc, 0:N],
                            start=(kc == 0), stop=(kc == 7)), [WuB[fc], yTB], [pVB[b]])

                def ff_stage_a(fc):
                    b = (k0 + fc) % 2
                    for kc in range(8):
                        S.op("pe", lambda e, b=b, kc=kc, fc=fc, N=N: e.matmul(
                            pG[b][:, 0:N], lhsT=Wu[:, kc, fc * 128:(fc + 1) * 128], rhs=yT[:, kc, 0:N],
                            start=(kc == 0), stop=(kc == 7)), [WuB[fc], yTB], [pGB[b]])
                    if fc >= 1:
                        ff_val(fc - 1)
                    g_, gB_ = gb[b], gbB[b]
                    S.op("act", lambda e, g_=g_, b=b, N=N: e.activation(out=g_[:, 2:2 + N], in_=pG[b][:, 0:N],
                                                                        func=AF.Copy), [pGB[b]], [gB_])
                    S.op("dve", lambda e, g_=g_, fc=fc: e.tensor_copy(out=g_[:, 0:2], in_=carry[:, fc, :]),
                         [carryB[fc]], [gB_])
                    S.op("dve", lambda e, g_=g_, fc=fc, N=N: e.tensor_copy(out=carry[:, fc, :], in_=g_[:, N:N + 2]),
                         [gB_], [carryB[fc]])
                    t_, tB_ = t1[b], t1B[b]
                    S.op("act", lambda e, g_=g_, t_=t_, fc=fc, N=N: e.activation(
                        out=t_[:, 0:N], in_=g_[:, 0:N], func=AF.Copy, scale=cw[:, fc, 0:1]),
                        [gB_, cwB], [tB_])

                def ff_stage_b(fc):
                    b = (k0 + fc) % 2
                    g_, gB_ = gb[b], gbB[b]
                    t_, tB_ = t1[b], t1B[b]
                    S.op("dve", lambda e, g_=g_, t_=t_, fc=fc, N=N: e.scalar_tensor_tensor(
                        out=t_[:, 0:N], in0=g_[:, 1:1 + N], scalar=cw[:, fc, 1:2], in1=t_[:, 0:N], op0=ALU.mult,
                        op1=ALU.add), [gB_, cwB, tB_], [tB_])
                    S.op("dve", lambda e, g_=g_, t_=t_, fc=fc, N=N: e.scalar_tensor_tensor(
                        out=t_[:, 0:N], in0=g_[:, 2:2 + N], scalar=cw[:, fc, 2:3], in1=t_[:, 0:N], op0=ALU.mult,
                        op1=ALU.add), [gB_, cwB, tB_], [tB_])
                    s_, sB_ = sl[b], slB[b]
                    S.op("act", lambda e, s_=s_, t_=t_, fc=fc, N=N: e.activation(
                        out=s_[:, 0:N], in_=t_[:, 0:N], func=AF.Silu, bias=cw[:, fc, 3:4], scale=1.0),
                        [tB_, cwB], [sB_])
                    S.op("dve", lambda e, s_=s_, b=b, fc=fc, N=N: e.tensor_tensor(
                        out=uT[:, fc, 0:N], in0=pV[b][:, 0:N], in1=s_[:, 0:N], op=ALU.mult), [pVB[b], sB_],
                        [uTB[fc]])

                k0 = k
                k += 22
                ff_stage_a(0)
                for fc in range(22):
                    if fc + 1 < 22:
                        ff_stage_a(fc + 1)
                    else:
                        ff_val(fc)
                    ff_stage_b(fc)
                have_next = gi + 1 < len(groups)
                ntn = groups[gi + 1][1] if have_next else 0
                t0n = groups[gi + 1][0] if have_next else 0
                nslots = {}
                if have_next:
                    R.kp += 1
                    for j in range(min(2, ntn)):
                        nslots[j] = na_load(R, Hs, t0n + j)
                ybn = {}

                def next_compute(j):
                    ybn[j] = na_compute(R, nslots[j], gi + 1, j)
                    if j + 2 < ntn:
                        nslots[j + 2] = na_load(R, Hs, t0n + j + 2)

                def hr_load(i):
                    tile = t0 + i
                    hs_ = (tk + i) % 2
                    S.dma("sp", hr[hs_][:], Hs[tile * 128:(tile + 1) * 128, :], writes=[hrB[hs_]])

                hr_load(0)
                for i in range(nt):
                    tile = t0 + i
                    rows = slice(tile * 128, (tile + 1) * 128)
                    hs = (tk + i) % 2
                    if i + 1 < nt:
                        hr_load(i + 1)
                    for half in range(2):
                        for fc in range(22):
                            S.op("pe", lambda e, half=half, fc=fc, i=i: e.matmul(
                                pD[half][:], lhsT=uT[:, fc, i * 128:(i + 1) * 128],
                                rhs=Wd[:, fc, half * 512:(half + 1) * 512], start=(fc == 0), stop=(fc == 21)),
                                [uTB[fc], WdB[fc]], [pDB[half]])
                    if i < ntn:
                        next_compute(i)
                    for half in range(2):
                        hv = hr[hs][:, half * 512:(half + 1) * 512]
                        S.op("dve", lambda e, hv=hv, half=half: e.tensor_tensor(out=hv, in0=pD[half][:], in1=hv,
                                                                                op=ALU.add),
                             [pDB[half], hrB[hs]], [hrB[hs]])
                    if not final:
                        if tile == 0:
                            S.op("dve", lambda e, hs=hs: e.memset(hr[hs][0:128 - NMETA, :], 0.0), [hrB[hs]], [hrB[hs]])
                        S.dma("sp", Hs[rows, :], hr[hs][:], reads=[hrB[hs]])
                    elif tile > 0:
                        S.op("act", lambda e, hs=hs: e.activation(out=fj[:], in_=hr[hs][:], func=AF.Square,
                                                                  accum_out=fs[:, 0:1]), [hrB[hs]], [fjB, fsB])
                        S.op("dve", lambda e: e.tensor_scalar(out=fs[:, 0:1], in0=fs[:, 0:1], scalar1=1.0 / D,
                                                              scalar2=EPS, op0=ALU.mult, op1=ALU.add), [fsB], [fsB])
                        S.op("act", lambda e: e.activation(out=fs[:, 0:1], in_=fs[:, 0:1], func=AF.Sqrt), [fsB], [fsB])
                        S.op("dve", lambda e: e.reciprocal(out=fs[:, 0:1], in_=fs[:, 0:1]), [fsB], [fsB])
                        S.op("dve", lambda e, hs=hs: e.scalar_tensor_tensor(
                            out=hr[hs][:], in0=hr[hs][:], scalar=fs[:, 0:1], in1=gf[:], op0=ALU.mult, op1=ALU.mult),
                            [hrB[hs], fsB, gfB], [hrB[hs]])
                        S.dma("sp", out[(tile - 1) * 128:tile * 128, :], hr[hs][:], reads=[hrB[hs]])
                    if i < ntn:
                        yT, yTB = norm_b_tile(R, ident, identB, ybn[i][0], ybn[i][1], i)
                tk += nt
                for j in range(nt, ntn):
                    next_compute(j)
                    yT, yTB = norm_b_tile(R, ident, identB, ybn[j][0], ybn[j][1], j)
            S.flush()

    phase_wprep()
    phase_proj(0, h0)
    phase_retention()
    phase_diff()
    phase_wout(0, h0, Hs)
    phase_ffn(0, final=(depth == 1))
    if depth > 1:
        phase_proj(1, Hs)
        phase_sb()
        phase_wout(1, Hs, Hs)
        phase_ffn(1, final=True)
    es.close()
    return nc, consts


def make_inputs(G, b, x, meta_tokens, mix_norm, ffn_norm, ffn_up, ffn_conv, ffn_conv_b, ffn_down, ab_w_in,
                ab_ret_norm, ab_diff_norm, ab_lam_q1, ab_lam_k1, ab_lam_q2, ab_lam_k2, ab_w_out, c_w_in, c_w_out,
                final_norm, consts, shared):
    P = 128 * (1 + 4 * G)
    f = np.float32
    h0 = np.zeros((P, D), f)
    h0[128 - NMETA:128] = meta_tokens
    h0[128:] = x[b, :P - 128]
    m = {"h0": h0}
    m.update(shared)
    return m


def make_shared(G, meta_tokens, mix_norm, ffn_norm, ffn_up, ffn_conv, ffn_conv_b, ffn_down, ab_w_in,
                ab_ret_norm, ab_diff_norm, ab_lam_q1, ab_lam_k1, ab_lam_q2, ab_lam_k2, ab_w_out, c_w_in, c_w_out,
                final_norm, consts):
    f = np.float32
    A = lambda a: np.ascontiguousarray(np.asarray(a, dtype=f))
    sh = {}
    sh["w_in0"] = A(ab_w_in[0])
    sh["w_out0"] = A(ab_w_out[0])
    sh["w_in1"] = A(c_w_in[0])
    sh["w_out1"] = A(c_w_out[0])
    nl = ffn_up.shape[0]
    for i in range(2):
        j = min(i, nl - 1)
        sh["up%d" % i] = A(ffn_up[j])
        sh["dn%d" % i] = A(ffn_down[j])
    g = np.zeros((128, 32), f)
    for i in range(2):
        j = min(i, nl - 1)
        g[:, i * 8:(i + 1) * 8] = np.asarray(mix_norm[j], f).reshape(8, 128).T
        g[:, 16 + i * 8:16 + (i + 1) * 8] = np.asarray(ffn_norm[j], f).reshape(8, 128).T
    sh["gains"] = g
    cw = np.zeros((128, 2, 22, 4), f)
    for i in range(2):
        j = min(i, nl - 1)
        for t in range(3):
            cw[:, i, :, t] = np.asarray(ffn_conv[j, t], f).reshape(22, 128).T
        cw[:, i, :, 3] = np.asarray(ffn_conv_b[j], f).reshape(22, 128).T
    sh["convw"] = cw
    sh["gret"] = A(np.broadcast_to(np.asarray(ab_ret_norm[0], f)[None, :], (128, 512)))
    sh["gdif"] = A(np.broadcast_to(np.asarray(ab_diff_norm[0], f)[None, :], (128, 512)))
    sh["gfin"] = A(np.broadcast_to(np.asarray(final_norm, f)[None, :], (128, D)))
    lv = np.stack([np.asarray(v[0], f) for v in (ab_lam_q1, ab_lam_k1, ab_lam_q2, ab_lam_k2)], 0)
    sh["lamv"] = A(np.broadcast_to(lv[None], (128, 4, 64)))
    for k in ("ident", "decT", "qdec", "kdec", "tri", "tris", "aq", "kx", "btab", "negU", "negOnes", "padb"):
        sh["c_" + k] = A(consts[k])
    return sh


_CACHE = {}


def run(G, depth, inputs, cores, debug=False):
    key = (G, depth, debug)
    if key not in _CACHE:
        _CACHE[key] = build(G, depth, debug)
    nc, consts = _CACHE[key]
    names = ["meta_tokens", "mix_norm", "ffn_norm", "ffn_up", "ffn_conv", "ffn_conv_b", "ffn_down", "ab_w_in",
             "ab_ret_norm", "ab_diff_norm", "ab_lam_q1", "ab_lam_k1", "ab_lam_q2", "ab_lam_k2", "ab_w_out",
             "c_w_in", "c_w_out", "final_norm"]
    args = [np.asarray(inputs[n]) for n in names]
    shared = make_shared(G, *args, consts)
    x = np.asarray(inputs["x"], np.float32)
    in_maps = [make_inputs(G, b, x, *args, consts, shared) for b in cores]
    res = run_bass_kernel_spmd(nc, in_maps, core_ids=list(range(len(cores))))
    return res


def kernel(**inputs):
    x = np.asarray(inputs["x"])
    B, SEQ, _ = x.shape
    G = SEQ // 512
    res = run(G, 2, inputs, list(range(B)))
    return np.stack([np.asarray(r["out"], np.float32) for r in res.results], 0)
```
